# Optimizing a Trainium2 kernel written in Bass

```python
import jax, jax.numpy as jnp
from jax import lax
import numpy as np

D_MODEL = 1024
BATCH = 8
SEQ = 2048
DEPTH = 4

N_MIXERS = 3
HEAD_DIM = 64
ROPE_THETA = 10000.0
RMS_EPS = 1e-6
NEG_INF = -1e30
GRID_W = 64

A_HEADS = D_MODEL // HEAD_DIM
A_GROUPS = ((128, 1), (512, 4), (2048, 16))
A_QBLOCK = 64
A_WIDTH = A_HEADS * HEAD_DIM

B_HEADS = D_MODEL // HEAD_DIM
NA_KH = 8
NA_KW = 16
NA_QCOLS = 16
NA_KCOLS = 32
B_WIDTH = B_HEADS * HEAD_DIM

C_Q_HEADS = D_MODEL // HEAD_DIM
C_KV_HEADS = 4
C_QBLOCK = 128
C_WIDTH = C_Q_HEADS * HEAD_DIM

D_FF = 4 * D_MODEL

kernel_name = "interleaved_dilated_neighbourhood_gqa_encoder"


def rms_norm(x, g):
    xf = x.astype(jnp.float32)
    y = xf * lax.rsqrt(jnp.mean(xf * xf, axis=-1, keepdims=True) + RMS_EPS)
    return (y * g.astype(jnp.float32)).astype(x.dtype)


def rope_angles(pos, dim):
    inv = 1.0 / (ROPE_THETA ** (jnp.arange(0, dim, 2, dtype=jnp.float32) / dim))
    return pos.astype(jnp.float32)[:, None] * inv[None, :]


def apply_rope(x, cos, sin):
    xf = x.astype(jnp.float32)
    x1, x2 = jnp.split(xf, 2, axis=-1)
    c = cos[:, None, :]
    s = sin[:, None, :]
    return jnp.concatenate([x1 * c - x2 * s, x2 * c + x1 * s], axis=-1).astype(x.dtype)


def dilated_group(q, k, v, dilation, half):
    B, S, H, Dh = q.shape
    L = S // dilation
    nb = -(-L // A_QBLOCK)
    Lp = nb * A_QBLOCK
    span_blocks = 1 + (2 * half) // A_QBLOCK
    span = span_blocks * A_QBLOCK
    qs = q.reshape(B, L, dilation, H, Dh)
    ks = k.reshape(B, L, dilation, H, Dh)
    vs = v.reshape(B, L, dilation, H, Dh)
    pad_q = ((0, 0), (0, Lp - L), (0, 0), (0, 0), (0, 0))
    pad_k = ((0, 0), (half, Lp - L + half), (0, 0), (0, 0), (0, 0))
    qb = jnp.pad(qs, pad_q).reshape(B, nb, A_QBLOCK, dilation, H, Dh)
    kb = jnp.pad(ks, pad_k).reshape(B, nb + span_blocks - 1, A_QBLOCK, dilation, H, Dh)
    vb = jnp.pad(vs, pad_k).reshape(B, nb + span_blocks - 1, A_QBLOCK, dilation, H, Dh)
    kw = jnp.concatenate([kb[:, s:s + nb] for s in range(span_blocks)], axis=2)
    vw = jnp.concatenate([vb[:, s:s + nb] for s in range(span_blocks)], axis=2)
    qi = np.arange(nb)[:, None, None] * A_QBLOCK + np.arange(A_QBLOCK)[None, :, None]
    kj = np.arange(nb)[:, None, None] * A_QBLOCK + np.arange(span)[None, None, :] - half
    mask = (kj >= 0) & (kj < L) & (np.abs(qi - kj) <= half)
    scale = Dh ** -0.5
    s = jnp.einsum('bnqrhe,bnkrhe->brhnqk', qb, kw, preferred_element_type=jnp.float32) * scale
    s = jnp.where(mask, s, NEG_INF)
    m = jnp.max(s, axis=-1, keepdims=True)
    p = jnp.exp(s - m)
    den = jnp.sum(p, axis=-1)
    o = jnp.einsum('brhnqk,bnkrhe->bnqrhe', p.astype(v.dtype), vw, preferred_element_type=jnp.float32)
    o = o / den.transpose(0, 3, 4, 1, 2)[..., None]
    lse = (m[..., 0] + jnp.log(den)).transpose(0, 3, 4, 1, 2)
    o = o.reshape(B, Lp, dilation, H, Dh)[:, :L].reshape(B, S, H, Dh)
    lse = lse.reshape(B, Lp, dilation, H)[:, :L].reshape(B, S, H)
    return o, lse


def mixer_a(h, w_in, w_out, cos, sin):
    B, S, _ = h.shape
    proj = (h @ w_in).reshape(B, S, len(A_GROUPS), 3, A_HEADS, HEAD_DIM)
    outs, lses = [], []
    for g, (window, dilation) in enumerate(A_GROUPS):
        q = apply_rope(proj[:, :, g, 0], cos, sin)
        k = apply_rope(proj[:, :, g, 1], cos, sin)
        v = proj[:, :, g, 2]
        o, lse = dilated_group(q, k, v, dilation, window // (2 * dilation))
        outs.append(o)
        lses.append(lse)
    wts = jax.nn.softmax(jnp.stack(lses, axis=0), axis=0)
    o = jnp.sum(wts[..., None] * jnp.stack(outs, axis=0), axis=0)
    return o.reshape(B, S, A_WIDTH).astype(h.dtype) @ w_out


def mixer_b(h, w_in, rpb, w_out):
    B, S, _ = h.shape
    rows = S // GRID_W
    kh = min(NA_KH, rows)
    qkv = (h @ w_in).reshape(B, rows, GRID_W, 3, B_HEADS, HEAD_DIM)
    q, k, v = qkv[:, :, :, 0], qkv[:, :, :, 1], qkv[:, :, :, 2]
    nqb = GRID_W // NA_QCOLS
    qc = np.arange(GRID_W).reshape(nqb, NA_QCOLS)
    kstart = np.clip(np.arange(nqb) * NA_QCOLS - (NA_KCOLS - NA_QCOLS) // 2, 0, GRID_W - NA_KCOLS)
    kc = kstart[:, None] + np.arange(NA_KCOLS)[None, :]
    cs = np.clip(qc - NA_KW // 2, 0, GRID_W - NA_KW)
    col_mask = (kc[:, None, :] >= cs[:, :, None]) & (kc[:, None, :] < cs[:, :, None] + NA_KW)
    dcol_idx = np.clip(kc[:, None, :] - qc[:, :, None] + NA_KW - 1, 0, 2 * NA_KW - 2)
    col_bias = rpb[:, :, dcol_idx]
    scale = HEAD_DIM ** -0.5

    def row_fn(r):
        rs = jnp.clip(r - kh // 2, 0, rows - kh)
        k_rows = lax.dynamic_slice_in_dim(k, rs, kh, axis=1)
        v_rows = lax.dynamic_slice_in_dim(v, rs, kh, axis=1)
        q_row = lax.dynamic_index_in_dim(q, r, axis=1, keepdims=False)
        q_cb = q_row.reshape(B, nqb, NA_QCOLS, B_HEADS, HEAD_DIM)
        k_cb = jnp.stack([k_rows[:, :, int(s0):int(s0) + NA_KCOLS] for s0 in kstart], axis=1)
        v_cb = jnp.stack([v_rows[:, :, int(s0):int(s0) + NA_KCOLS] for s0 in kstart], axis=1)
        s = jnp.einsum('bjqhd,bjakhd->bhjqak', q_cb, k_cb, preferred_element_type=jnp.float32) * scale
        drow_idx = rs + jnp.arange(kh) - r + NA_KH - 1
        bias = col_bias[:, drow_idx].transpose(0, 2, 3, 1, 4)
        s = s + bias[None].astype(jnp.float32)
        s = jnp.where(col_mask[:, :, None, :], s, NEG_INF)
        p = jax.nn.softmax(s.reshape(B, B_HEADS, nqb, NA_QCOLS, kh * NA_KCOLS), axis=-1)
        o = jnp.einsum('bhjqn,bjnhd->bjqhd', p.astype(v.dtype),
                       v_cb.reshape(B, nqb, kh * NA_KCOLS, B_HEADS, HEAD_DIM))
        return o.reshape(B, GRID_W, B_HEADS, HEAD_DIM)

    out = lax.map(row_fn, jnp.arange(rows, dtype=jnp.int32))
    out = out.transpose(1, 0, 2, 3, 4).reshape(B, S, B_WIDTH)
    return out @ w_out


def mixer_c(h, w_in, q_norm, k_norm, w_out, cos, sin):
    B, S, _ = h.shape
    G = C_Q_HEADS // C_KV_HEADS
    proj = h @ w_in
    nq = C_Q_HEADS * HEAD_DIM
    nk = C_KV_HEADS * HEAD_DIM
    q = proj[..., :nq].reshape(B, S, C_Q_HEADS, HEAD_DIM)
    k = proj[..., nq:nq + nk].reshape(B, S, C_KV_HEADS, HEAD_DIM)
    v = proj[..., nq + nk:].reshape(B, S, C_KV_HEADS, HEAD_DIM)
    q = apply_rope(rms_norm(q, q_norm), cos, sin)
    k = apply_rope(rms_norm(k, k_norm), cos, sin)
    nqb = S // C_QBLOCK
    qb = q.reshape(B, nqb, C_QBLOCK, C_KV_HEADS, G, HEAD_DIM).transpose(1, 0, 2, 3, 4, 5)
    scale = HEAD_DIM ** -0.5

    def block_fn(qblk):
        s = jnp.einsum('bqkgd,bskd->bkgqs', qblk, k, preferred_element_type=jnp.float32) * scale
        p = jax.nn.softmax(s, axis=-1)
        return jnp.einsum('bkgqs,bskd->bqkgd', p.astype(v.dtype), v)

    out = lax.map(block_fn, qb)
    out = out.transpose(1, 0, 2, 3, 4, 5).reshape(B, S, C_WIDTH)
    return out @ w_out


def squared_relu_mlp(h, w_up, w_down):
    u = h @ w_up
    return jnp.square(jax.nn.relu(u)) @ w_down


def setup_inputs(seed: int = 0) -> dict:
    key = jax.random.key(seed)
    ks = iter(jax.random.split(key, 10 * DEPTH + 4))

    def nrm(shape, scale):
        return jax.random.normal(next(ks), shape, jnp.float32) * scale

    def gain(n):
        return 1.0 + nrm((n,), 0.05)

    p = {"x": nrm((BATCH, SEQ, D_MODEL), 1.0)}
    for i in range(DEPTH):
        kind = i % N_MIXERS
        p[f"l{i}_attn_norm"] = gain(D_MODEL)
        if kind == 0:
            p[f"l{i}_w_in"] = nrm((D_MODEL, len(A_GROUPS) * 3 * A_WIDTH), D_MODEL ** -0.5)
            p[f"l{i}_w_out"] = nrm((A_WIDTH, D_MODEL), A_WIDTH ** -0.5)
        elif kind == 1:
            p[f"l{i}_w_in"] = nrm((D_MODEL, 3 * B_WIDTH), D_MODEL ** -0.5)
            p[f"l{i}_rpb"] = nrm((B_HEADS, 2 * NA_KH - 1, 2 * NA_KW - 1), 0.1)
            p[f"l{i}_w_out"] = nrm((B_WIDTH, D_MODEL), B_WIDTH ** -0.5)
        else:
            p[f"l{i}_w_in"] = nrm((D_MODEL, (C_Q_HEADS + 2 * C_KV_HEADS) * HEAD_DIM), D_MODEL ** -0.5)
            p[f"l{i}_q_norm"] = gain(HEAD_DIM)
            p[f"l{i}_k_norm"] = gain(HEAD_DIM)
            p[f"l{i}_w_out"] = nrm((C_WIDTH, D_MODEL), C_WIDTH ** -0.5)
        p[f"l{i}_mlp_norm"] = gain(D_MODEL)
        p[f"l{i}_w_up"] = nrm((D_MODEL, D_FF), D_MODEL ** -0.5)
        p[f"l{i}_w_down"] = nrm((D_FF, D_MODEL), D_FF ** -0.5)
    p["final_norm"] = gain(D_MODEL)
    return p


def reference(x,
              l0_attn_norm, l0_w_in, l0_w_out, l0_mlp_norm, l0_w_up, l0_w_down,
              l1_attn_norm, l1_w_in, l1_rpb, l1_w_out, l1_mlp_norm, l1_w_up, l1_w_down,
              l2_attn_norm, l2_w_in, l2_q_norm, l2_k_norm, l2_w_out, l2_mlp_norm, l2_w_up, l2_w_down,
              l3_attn_norm, l3_w_in, l3_w_out, l3_mlp_norm, l3_w_up, l3_w_down,
              final_norm):
    S = x.shape[1]
    t = jnp.arange(S, dtype=jnp.int32)
    ang_a = rope_angles(t, HEAD_DIM)
    ang_c = jnp.concatenate([rope_angles(t // GRID_W, HEAD_DIM // 2),
                             rope_angles(t % GRID_W, HEAD_DIM // 2)], axis=-1)
    cos_a, sin_a = jnp.cos(ang_a), jnp.sin(ang_a)
    cos_c, sin_c = jnp.cos(ang_c), jnp.sin(ang_c)

    attn_norms = (l0_attn_norm, l1_attn_norm, l2_attn_norm, l3_attn_norm)
    mixer_params = ((l0_w_in, l0_w_out),
                    (l1_w_in, l1_rpb, l1_w_out),
                    (l2_w_in, l2_q_norm, l2_k_norm, l2_w_out),
                    (l3_w_in, l3_w_out))
    mlp_norms = (l0_mlp_norm, l1_mlp_norm, l2_mlp_norm, l3_mlp_norm)
    w_ups = (l0_w_up, l1_w_up, l2_w_up, l3_w_up)
    w_downs = (l0_w_down, l1_w_down, l2_w_down, l3_w_down)

    for i in range(DEPTH):
        kind = i % N_MIXERS
        h = rms_norm(x, attn_norms[i])
        if kind == 0:
            y = mixer_a(h, *mixer_params[i], cos_a, sin_a)
        elif kind == 1:
            y = mixer_b(h, *mixer_params[i])
        else:
            y = mixer_c(h, *mixer_params[i], cos_c, sin_c)
        x = x + y
        x = x + squared_relu_mlp(rms_norm(x, mlp_norms[i]), w_ups[i], w_downs[i])
    return rms_norm(x, final_norm)
```

```python
from contextlib import ExitStack
import numpy as np
import concourse.bass as bass
import concourse.mybir as mybir
from concourse.bass_utils import run_bass_kernel_spmd

F32 = mybir.dt.float32
BF16 = mybir.dt.bfloat16
AF = mybir.ActivationFunctionType
ALU = mybir.AluOpType
AX = mybir.AxisListType

S = 2048
D = 1024
NTT = 16
NCH = 8
DEPTH = 4
EPS = 1e-6
DIL = (1, 4, 16)
NEG = -1e30

ENGS = ("pe", "act", "dve", "pool", "sp")


class Buf:
    __slots__ = ("name", "w", "rs", "excl")

    def __init__(self, name="", excl=False):
        self.name = name
        self.w = None
        self.rs = []
        self.excl = excl


class Op:
    __slots__ = ("eng", "fn", "deps", "signal", "tok", "is_dma")

    def __init__(self, eng, fn, is_dma):
        self.eng = eng
        self.fn = fn
        self.deps = []
        self.signal = False
        self.tok = None
        self.is_dma = is_dma


class Prog:
    def __init__(self, nc):
        self.nc = nc
        self.ops = {e: [] for e in ENGS}
        self.stack = ExitStack()
        self.sems = {}
        self.dma_cnt = {}
        self.pending_dmas = []
        self.n_ops = 0

    def sem(self, name):
        if name not in self.sems:
            self.sems[name] = self.stack.enter_context(self.nc.semaphore(name))
        return self.sems[name]

    def sbuf(self, name, shape, dt):
        return self.stack.enter_context(self.nc.sbuf_tensor(name, list(shape), dt))

    def psum(self, name, shape, dt):
        return self.stack.enter_context(self.nc.psum_tensor(name, list(shape), dt))

    def op(self, eng, fn, reads=(), writes=(), dma_key=None):
        is_dma = dma_key is not None
        o = Op(eng, fn, is_dma)
        need = {}
        for b in reads:
            p = b.w
            if p is not None:
                need[id(p)] = (p, True)
            if b.excl:
                for r in b.rs:
                    if r.eng != eng and id(r) not in need:
                        need[id(r)] = (r, False)
        for b in writes:
            p = b.w
            if p is not None and id(p) not in need:
                need[id(p)] = (p, False)
            for r in b.rs:
                if id(r) not in need:
                    need[id(r)] = (r, False)
        for p, raw in need.values():
            if p.eng == eng and not p.is_dma and not is_dma:
                if not raw:
                    continue
                if eng == "pe":
                    continue
            p.signal = True
            o.deps.append(p)
        for b in reads:
            b.rs.append(o)
        for b in writes:
            b.w = o
            b.rs = []
        if is_dma:
            self.sem(dma_key)
            c = self.dma_cnt.get(dma_key, 0) + 16
            self.dma_cnt[dma_key] = c
            o.tok = (dma_key, c)
            o.signal = True
            self.pending_dmas.append(o)
        self.ops[eng].append(o)
        self.n_ops += 1
        return o

    def barrier(self):
        lasts = []
        for e in ENGS:
            for o in reversed(self.ops[e]):
                if not o.is_dma and o.fn is not None:
                    o.signal = True
                    lasts.append(o)
                    break
        dmas = self.pending_dmas
        self.pending_dmas = []
        for e in ENGS:
            b = Op(e, None, False)
            b.deps = [o for o in lasts if o.eng != e] + list(dmas)
            self.ops[e].append(b)

    def emit(self):
        nc = self.nc
        for e in ENGS:
            self.sem("s_" + e)
            c = 0
            for o in self.ops[e]:
                if o.is_dma or o.fn is None:
                    continue
                if o.signal:
                    c += 1
                    o.tok = ("s_" + e, c)
        sems = self.sems

        def run(e, engobj):
            waited = {}
            for o in self.ops[e]:
                for p in o.deps:
                    k, v = p.tok
                    if waited.get(k, 0) >= v:
                        continue
                    waited[k] = v
                    engobj.wait_ge(sems[k], v)
                if o.fn is None:
                    continue
                ins = o.fn(engobj)
                if o.signal:
                    ins.then_inc(sems[o.tok[0]], 16 if o.is_dma else 1)

        with nc.Block() as block:
            @block.tensor
            def _(eng):
                run("pe", eng)

            @block.scalar
            def _(eng):
                run("act", eng)

            @block.vector
            def _(eng):
                run("dve", eng)

            @block.gpsimd
            def _(eng):
                run("pool", eng)

            @block.sync
            def _(eng):
                run("sp", eng)

    def close(self):
        self.stack.close()


def tokv(ap2d, d, r, i0, i1):
    if d == 1:
        return ap2d[:, i0:i1]
    return ap2d.rearrange("p (i s) -> p i s", s=d)[:, i0:i1, r]


ARENA_BYTES = 76 * 1024


def build_program(layer_ids, do_final, dbg=None):
    dbg = dbg or {}
    nc = bass.Bass("TRN2", target_bir_lowering=False)
    P = Prog(nc)

    def din(name, shape):
        return nc.dram_tensor(name, list(shape), F32, kind="ExternalInput").ap()

    xT_d = din("xT", [D, S])
    gains_d = din("gains", [128, 9 * NCH])
    ident_d = din("ident", [128, 128])
    band_d = din("band", [128, 256])
    ropeA_d = din("ropeA", [6, 128, NTT * 32])
    ropeC_d = din("ropeC", [2, 128, NTT * 32])
    cgain_d = din("cgain", [128, 384])
    biasB_d = din("biasB", [8, 128, 2 * 2 * 16 * 64])
    w_d = {}
    for li in layer_ids:
        kind = li % 3
        nsl, ncols = ((24, 384), (8, 384), (4, 448))[kind]
        w_d[li] = dict(
            w_in=din(f"w{li}_in", [nsl, 128, NCH * ncols]),
            w_out=din(f"w{li}_out", [8, 128, D]),
            w_up=din(f"w{li}_up", [8, 128, NCH * 512]),
            w_dn=din(f"w{li}_dn", [8, 128, 4 * D]),
        )
    outT_d = nc.dram_tensor("outT", [D, S], F32, kind="ExternalOutput").ap()

    xT = P.sbuf("xT_sb", [128, NCH, S], F32)
    hT = P.sbuf("hT_sb", [128, NCH, S], BF16)
    arena = P.sbuf("arena", [128, ARENA_BYTES // 2], BF16)
    ident = P.sbuf("ident_sb", [128, 128], BF16)
    band = P.sbuf("band_sb", [128, 256], BF16)
    ones32 = P.sbuf("ones32", [128, 128], F32)
    gains = P.sbuf("gains_sb", [128, 9 * NCH], F32)
    sq = [P.sbuf(f"sq{i}", [128, 512], F32) for i in range(2)]
    rstd = P.sbuf("rstd", [128, 512], F32)
    NE, NQ, NOV = 6, 4, 3
    LATE_DELAY = dbg.get("late", 2)
    TDEL = dbg.get("tdel", 2)
    COPY_ENG = dbg.get("copy_eng", "dve")
    MASK_ENG = dbg.get("mask_eng", "pool")
    ROPE2_ENG = dbg.get("rope2_eng", "pool")
    Et = [P.sbuf(f"Et{i}", [128, 512], BF16) for i in range(NE)]
    qktm = [P.sbuf(f"qktm{i}", [128, 384], BF16) for i in range(NQ)]
    tmpf = [[P.sbuf(f"tmp{i}_{k}", [128, 384], F32) for k in range(3)] for i in range(2)]
    ssq = [P.sbuf(f"ssq{i}", [128, 8], F32) for i in range(2)]
    rc = [P.sbuf(f"rc{i}", [128, 512], F32) for i in range(2)]

    B_sq = [Buf() for _ in range(2)]
    B_rstd = Buf()
    B_Et = [Buf() for _ in range(NE)]
    B_Pt = [Buf() for _ in range(NE)]
    B_qktm = [Buf() for _ in range(NQ)]
    B_tmp = [Buf() for _ in range(2)]
    B_rc = [Buf() for _ in range(2)]
    B_xT = [[Buf() for _ in range(4)] for _ in range(NCH)]
    B_hT = [Buf() for _ in range(4)]
    B_const = Buf()
    B_gains, B_ident, B_band = Buf(), Buf(), Buf()
    B_out = Buf()

    PJ = [P.psum(f"pj{i}", [128, 512], F32) for i in range(2)]
    TR = [P.psum(f"tr{i}", [128, 1024], BF16) for i in range(1)]
    STB = [P.psum(f"st{i}", [128, 512], F32) for i in range(2)]
    OVB = [P.psum(f"ov{i}", [128, 512], F32) for i in range(NOV)]
    B_PJ = [Buf(excl=True) for _ in range(2)]
    B_TR = [Buf(excl=True) for _ in range(1)]
    B_ST = [Buf(excl=True) for _ in range(2)]
    B_OV = [Buf(excl=True) for _ in range(NOV)]
    ctr = dict(pj=0, tr=0, st=0, ov=0, et=0, qk=0, tmp=0, rc=0, sq=0, g4=0)

    def nxt(k, n):
        v = ctr[k] % n
        ctr[k] += 1
        return v

    class Arena:
        def __init__(self):
            self.off = 0

        def reset(self):
            self.off = 0

        def take(self, free_shape, dt):
            n = int(np.prod(free_shape))
            size = n * (4 if dt == F32 else 2)
            assert self.off + size <= ARENA_BYTES, (self.off, size)
            a = arena[:, self.off // 2:(self.off + size) // 2]
            self.off += (size + 63) // 64 * 64
            if dt == F32:
                a = a.bitcast(F32)
            if len(free_shape) == 2:
                a = a.rearrange("p (a b) -> p a b", a=free_shape[0])
            elif len(free_shape) == 3:
                a = a.rearrange("p (a b c) -> p a b c", a=free_shape[0], b=free_shape[1])
            return a

    AR = Arena()

    def dma(eng, out, in_, key, writes=(), reads=()):
        return P.op(eng, lambda e, o=out, i=in_: e.dma_start(out=o, in_=i), reads=reads, writes=writes, dma_key=key)

    def mm(out, lhsT, rhs, start, stop, reads, writes, skip=False):
        return P.op("pe", lambda e, o=out, l=lhsT, r=rhs, s0=start, s1=stop, sk=skip:
                    e.matmul(o, l, r, start=s0, stop=s1, skip_group_check=sk), reads=reads, writes=writes)

    def act(out, in_, func, reads, writes, scale=1.0, bias=0.0):
        return P.op("act", lambda e, o=out, i=in_, f=func, s=scale, b=bias:
                    e.activation(out=o, in_=i, func=f, bias=b, scale=s), reads=reads, writes=writes)

    def tt(out, in0, in1, op, reads, writes, eng="dve"):
        return P.op(eng, lambda e, o=out, a=in0, b=in1, p=op: e.tensor_tensor(out=o, in0=a, in1=b, op=p),
                    reads=reads, writes=writes)

    def recip(out, in_, reads, writes):
        return P.op("dve", lambda e, o=out, i=in_: e.reciprocal(out=o, in_=i), reads=reads, writes=writes)

    xT_dv = xT_d.rearrange("(c p) t -> p c t", p=128)
    for c in range(NCH):
        dma("sp", xT[:, c, :], xT_dv[:, c, :], f"d_x{c}", writes=B_xT[c])
    dma("sp", gains[:], gains_d, "d_c", writes=[B_gains])
    dma("pool", ident[:], ident_d, "d_c2", writes=[B_ident])
    dma("pool", band[:], band_d, "d_c3", writes=[B_band])
    P.op("dve", lambda e: e.memset(ones32[:], 1.0), writes=[B_const])

    def rmsnorm(gi, to_x):
        for tc in range(4):
            tcs = slice(tc * 512, (tc + 1) * 512)
            k = nxt("pj", 2)
            for c in range(NCH):
                s = nxt("sq", 2)
                if c % 2 == 0:
                    act(sq[s][:], xT[:, c, tcs], AF.Square, reads=[B_xT[c][tc]], writes=[B_sq[s]])
                else:
                    tt(sq[s][:], xT[:, c, tcs], xT[:, c, tcs], ALU.mult, reads=[B_xT[c][tc]], writes=[B_sq[s]], eng="pool")
                mm(PJ[k][:], ones32[:], sq[s][:], c == 0, c == NCH - 1, reads=[B_sq[s], B_const], writes=[B_PJ[k]])
            act(rstd[:], PJ[k][:], AF.Sqrt, reads=[B_PJ[k]], writes=[B_rstd], scale=1.0 / D, bias=EPS)
            recip(rstd[:], rstd[:], reads=[B_rstd], writes=[B_rstd])
            for c in range(NCH):
                dst = xT[:, c, tcs] if to_x else hT[:, c, tcs]
                wr = [B_xT[c][tc]] if to_x else [B_hT[tc]]
                P.op("dve", lambda e, o=dst, a=xT[:, c, tcs], g=gains[:, gi * NCH + c:gi * NCH + c + 1]:
                     e.scalar_tensor_tensor(out=o, in0=a, scalar=g, in1=rstd[:], op0=ALU.mult, op1=ALU.mult),
                     reads=[B_xT[c][tc], B_rstd, B_gains], writes=wr)

    def attention_layer(li):
        kind = li % 3
        wd = w_d[li]
        nsl, ncols = ((24, 384), (8, 384), (4, 448))[kind]
        nT = 3 if kind == 2 else 2
        tb = 2 if kind == 2 else 4
        nqk = 384 if kind == 2 else 256
        nbuf = 2 if kind == 2 else 1
        AR.reset()
        wsl = [AR.take((NCH, ncols), BF16) for _ in range(2)]
        nS = nT + 1
        qkT = [AR.take((nS, S), BF16) for _ in range(nbuf)]
        Vt = [AR.take((NTT, 192), BF16) for _ in range(nbuf)]
        oTp = [AR.take((S,), BF16) for _ in range(2)]
        woutp = [AR.take((D,), BF16) for _ in range(2)]
        B_wsl = [Buf() for _ in range(2)]
        B_qkT = [[Buf() for _ in range(4)] for _ in range(nbuf)]
        B_Vt = [[Buf() for _ in range(NTT)] for _ in range(nbuf)]
        B_oTp = [[Buf() for _ in range(4)] for _ in range(2)]
        B_wout = [Buf() for _ in range(2)]
        B_lc = Buf()
        if kind == 0:
            acc = [AR.take((S,), F32) for _ in range(2)]
            B_acc = [Buf() for _ in range(2)]
            tabsf = AR.take((6 * NTT * 32,), F32)
            tabs = tabsf.rearrange("p (a b c) -> p a b c", a=6, b=NTT)
            for i in range(6):
                dma("sp", tabsf[:, i * 512:(i + 1) * 512], ropeA_d[i], "d_lc", writes=[B_lc])
        elif kind == 1:
            biasf = [AR.take((4096,), BF16) for _ in range(2)]
            bias = [b.rearrange("p (h t u e) -> p h t u e", h=2, t=2, u=16) for b in biasf]
            B_bias = [Buf() for _ in range(2)]
        else:
            tabsf = AR.take((2 * NTT * 32,), F32)
            tabs = tabsf.rearrange("p (a b c) -> p a b c", a=2, b=NTT)
            cg = AR.take((384,), F32)
            for i in range(2):
                dma("sp", tabsf[:, i * 512:(i + 1) * 512], ropeC_d[i], "d_lc", writes=[B_lc])
            dma("sp", cg, cgain_d, "d_lc", writes=[B_lc])
        for bi in range(nbuf):
            P.op("dve", lambda e, v=Vt[bi]: e.memset(v[:, :, 64:128], 1.0), writes=B_Vt[bi])
            P.op("pool", lambda e, q=qkT[bi]: e.memset(q[64:128, nT - 1, :], 0.0), writes=B_qkT[bi])
            P.op("pool", lambda e, q=qkT[bi]: e.memset(q[0:64, nT, :], 0.0), writes=B_qkT[bi])

        def load_w(s):
            sl = s % 2
            dma("pool", wsl[sl], wd["w_in"][s].rearrange("p (c n) -> p c n", n=ncols), f"d_wsl{sl}", writes=[B_wsl[sl]])

        def load_wout(pi):
            sl = pi % 2
            dma("pool", woutp[sl], wd["w_out"][pi], f"d_wo{sl}", writes=[B_wout[sl]])

        pend = []

        late = []
        if kind == 2:
            st_pool = [(STB[0], B_ST[0]), (STB[1], B_ST[1]), (PJ[0], B_PJ[0]), (PJ[1], B_PJ[1])]
        else:
            st_pool = [(STB[0], B_ST[0]), (STB[1], B_ST[1])]
        infl = []
        DEPTH_IT = dbg.get("depth", 4 if kind == 2 else 2)

        def step():
            it = pend.pop(0)
            it["st"]()
            infl.append(it)
            if len(infl) > DEPTH_IT:
                infl.pop(0)["pv"]()

        def force_drain(pred):
            lastm = -1
            for i_, ent in enumerate(late):
                if any(pred(it) for it in ent[1]):
                    lastm = i_
            for _ in range(lastm + 1):
                pend.extend(late.pop(0)[1])
            while any(pred(it) for it in pend):
                step()
            while infl and any(pred(it) for it in infl):
                infl.pop(0)["pv"]()

        def drain_all():
            while late:
                pend.extend(late.pop(0)[1])
            while pend:
                step()
            while infl:
                infl.pop(0)["pv"]()

        def special(fn, unit):
            return dict(st=lambda: None, pv=fn, unit=unit, minblk=0)

        def mk_item(st_args, mask_fn, pv_list, done_fn, unit, minblk):
            state = {}

            def st():
                k = nxt("st", 2)
                lhsT, rhs, c0, w, rd = st_args
                mm(STB[k][:, c0:c0 + w], lhsT, rhs, True, True, reads=rd, writes=[B_ST[k]])
                ei = nxt("et", NE)
                act(Et[ei][:, c0:c0 + w], STB[k][:, c0:c0 + w], AF.Exp, reads=[B_ST[k]], writes=[B_Et[ei]], scale=0.125)
                src, Bsrc = Et[ei], B_Et[ei]
                if mask_fn is not None:
                    mask_fn(Et[ei], Pt[ei], B_Et[ei], B_Pt[ei])
                    src, Bsrc = Pt[ei], B_Pt[ei]
                state["src"] = (src, Bsrc)

            def pv():
                src, Bsrc = state["src"]
                for (lhsT, pc0, pc1, ob, oc, start, stop, rd) in pv_list:
                    mm(OVB[ob][:, oc:oc + (pc1 - pc0)], lhsT, src[:, pc0:pc1], start, stop,
                       reads=rd + [Bsrc], writes=[B_OV[ob]], skip=True)
                if done_fn is not None:
                    done_fn()
            return dict(st=st, pv=pv, unit=unit, minblk=minblk)

        bankmaps = {}
        ov_open = set()
        ov_lru = list(range(NOV))

        def ov_alloc():
            for p in ov_lru:
                if p not in ov_open:
                    ov_lru.remove(p)
                    ov_lru.append(p)
                    ov_open.add(p)
                    return p
            raise AssertionError("no free OV bank")

        def ov_release(p):
            ov_open.discard(p)

        def mk_item_lazy(st_args, mask_fn, pv_list, done_fn, unit, minblk, bank_of):
            state = {}

            def st():
                stb, Bstb = st_pool[nxt("st", len(st_pool))]
                lhsT, rhs, c0, w, rd = st_args
                mm(stb[:, c0:c0 + w], lhsT, rhs, True, mask_fn is None, reads=rd, writes=[Bstb])
                if mask_fn is not None:
                    mrhs, mrd = mask_fn
                    mm(stb[:, c0:c0 + w], ident[:], mrhs, False, True, reads=mrd + [B_ident], writes=[Bstb])
                ei = nxt("et", NE)
                act(Et[ei][:, c0:c0 + w], stb[:, c0:c0 + w], AF.Exp, reads=[Bstb], writes=[B_Et[ei]], scale=0.125)
                state["src"] = (Et[ei], B_Et[ei])

            def pv():
                src, Bsrc = state["src"]
                for (lhsT, pc0, pc1, lb, oc, start, stop, rd) in pv_list:
                    nb = lb[1]
                    if start:
                        bank_of[nb] = ov_alloc()
                    ob = bank_of[nb]
                    mm(OVB[ob][:, oc:oc + (pc1 - pc0)], lhsT, src[:, pc0:pc1], start, stop,
                       reads=rd + [Bsrc], writes=[B_OV[ob]], skip=True)
                if done_fn is not None:
                    done_fn()
            return dict(st=st, pv=pv, unit=unit, minblk=minblk)

        def normalise(src, hh, dst, w, reads, writes):
            ri = nxt("rc", 2)
            orow = slice(64 * hh, 64 * hh + 64)
            drow = slice(64 * (1 - hh), 64 * (1 - hh) + 64)
            if kind == 2:
                recip(rc[ri][orow, 0:w], src[drow, :], reads=reads, writes=[B_rc[ri]])
            else:
                act(rc[ri][orow, 0:w], src[drow, :], AF.Ln, reads=reads, writes=[B_rc[ri]])
                act(rc[ri][orow, 0:w], rc[ri][orow, 0:w], AF.Exp, reads=[B_rc[ri]], writes=[B_rc[ri]], scale=-1.0)
            tt(dst[orow, :], src[orow, :], rc[ri][orow, 0:w], ALU.mult, reads=reads + [B_rc[ri]], writes=writes)

        def wout_items(pi, unit):
            sl = pi % 2
            res = []
            for tc in range(4):
                for mh in range(2):
                    def fn(tc=tc, mh=mh):
                        for m in range(mh * 4, mh * 4 + 4):
                            b4 = nxt("g4", 2 + NOV)
                            while b4 >= 2 and (b4 - 2) in ov_open:
                                b4 = nxt("g4", 2 + NOV)
                            bank, Bb = (STB + OVB)[b4], (B_ST + B_OV)[b4]
                            mm(bank[:], woutp[sl][:, m * 128:(m + 1) * 128], oTp[sl][:, tc * 512:(tc + 1) * 512], True, True,
                               reads=[B_wout[sl], B_oTp[sl][tc]], writes=[Bb])
                            tt(xT[:, m, tc * 512:(tc + 1) * 512], bank[:], xT[:, m, tc * 512:(tc + 1) * 512], ALU.add,
                               reads=[Bb, B_xT[m][tc]], writes=[B_xT[m][tc]])
                    res.append(special(fn, unit))
            return res

        def rope(x3, nh, cosb, sinb, t1, t2, dst, reads, B_t, B_dst):
            n = 2 * nh * 32
            t1v = t1[:, 0:n].rearrange("p (h e) -> p h e", e=32)
            t2v = t2[:, 0:n].rearrange("p (h e) -> p h e", e=32)
            tt(t1v, x3, cosb, ALU.mult, reads=reads, writes=[B_t])
            tt(t2v, x3, sinb, ALU.mult, reads=reads, writes=[B_t])
            t1q = t1[:, 0:n].rearrange("p (h t e) -> p h t e", t=2, e=32)
            t2q = t2[:, 0:n].rearrange("p (h t e) -> p h t e", t=2, e=32)
            dq = dst[:, 0:n].rearrange("p (h t e) -> p h t e", t=2, e=32)
            tt(dq[:, :, 0, :], t1q[:, :, 0, :], t2q[:, :, 1, :], ALU.subtract, reads=[B_t], writes=[B_dst], eng=ROPE2_ENG)
            tt(dq[:, :, 1, :], t1q[:, :, 1, :], t2q[:, :, 0, :], ALU.add, reads=[B_t], writes=[B_dst], eng=ROPE2_ENG)

        def run_unit(ui, U):
            sl = U["s"] % 2
            bi = U["buf"]
            st = dict(trk=None, qis={})

            def transposes(j):
                qi = st["qis"].pop(j)
                jj = j % tb
                if jj == 0:
                    st["trk"] = nxt("tr", 1)
                trk = st["trk"]
                trv = TR[trk][:, 0:nT * tb * 128].rearrange("p (n x) -> p n x", n=nT)
                for blk in range(nT):
                    P.op("pe", lambda e, o=trv[:, blk, jj * 128:(jj + 1) * 128], i=qktm[qi][:, blk * 128:(blk + 1) * 128]:
                         e.transpose(o, i, ident[:]), reads=[B_qktm[qi], B_ident], writes=[B_TR[trk]])
                if jj == tb - 1:
                    jb0 = (j - tb + 1) * 128
                    cs_ = slice(jb0, jb0 + tb * 128)
                    Bq = [B_qkT[bi][jb0 // 512]]
                    for (o_, i_) in ((qkT[bi][:, 0:nT - 1, cs_], trv[:, 0:nT - 1, :]),
                                     (qkT[bi][0:64, nT - 1, cs_], trv[0:64, nT - 1, :]),
                                     (qkT[bi][64:128, nT, cs_], trv[64:128, nT - 1, :])):
                        if COPY_ENG == "act":
                            act(o_, i_, AF.Copy, reads=[B_TR[trk]], writes=Bq)
                        else:
                            P.op("dve", lambda e, o=o_, i=i_: e.tensor_copy(out=o, in_=i), reads=[B_TR[trk]], writes=Bq)
                    if (j + 1) % 4 == 0:
                        late.append([LATE_DELAY, U["ready"](j // 4)])

            def tile(j):
                k = nxt("pj", 2)
                for c in range(NCH):
                    mm(PJ[k][:, 0:ncols], U["tok"](j, c), wsl[sl][:, c, :], c == 0, c == NCH - 1,
                       reads=B_hT + [B_wsl[sl]], writes=[B_PJ[k]])
                vdst = Vt[bi][:, j, :].rearrange("p (b e) -> p b e", e=64)[:, 0:3:2, :]
                if kind != 2:
                    vsrc = PJ[k][:, nqk:nqk + 128].rearrange("p (b e) -> p b e", e=64)
                else:
                    vsrc = PJ[k][:, 384:448].unsqueeze(1).to_broadcast([128, 2, 64])
                act(vdst, vsrc, AF.Copy, reads=[B_PJ[k]], writes=[B_Vt[bi][j]])
                qi = nxt("qk", NQ)
                st["qis"][j] = qi
                U["evac"](j, PJ[k], B_PJ[k], qktm[qi], B_qktm[qi])
                if j >= TDEL:
                    transposes(j - TDEL)

            for j in range(NTT):
                for ent in late:
                    ent[0] -= 1
                while late and late[0][0] <= 0:
                    pend.extend(late.pop(0)[1])
                if j % 4 == 0:
                    force_drain(U["war"](j // 4))
                kk = 3 if len(pend) > 20 else 2
                for _ in range(min(kk, len(pend))):
                    step()
                tile(j)
            for jj_ in range(NTT - TDEL, NTT):
                for _ in range(min(2, len(pend))):
                    step()
                transposes(jj_)

        if kind == 0:
            units = []
            for hp in range(dbg.get("units", 8)):
                for g in range(3):
                    ui = len(units)
                    d = DIL[g]
                    L = S // d
                    T = L // 128

                    def tok_fn(j, c, d=d, L=L):
                        u0 = j * 128
                        r, i0 = u0 // L, u0 % L
                        return tokv(hT[:, c, :], d, r, i0, i0 + 128)

                    def evac(j, pj, Bpj, qk, Bqk, g=g):
                        ti = nxt("tmp", 2)
                        x3 = pj[:, 0:256].rearrange("p (h e) -> p h e", e=32)
                        cosb = tabs[:, 2 * g, j, :].unsqueeze(1).to_broadcast([128, 8, 32])
                        sinb = tabs[:, 2 * g + 1, j, :].unsqueeze(1).to_broadcast([128, 8, 32])
                        rope(x3, 4, cosb, sinb, tmpf[ti][0], tmpf[ti][1], qk, [Bpj, B_lc], B_tmp[ti], Bqk)

                    by_batch = {b: [] for b in range(4)}
                    by_bh = {(b, h): [] for b in range(4) for h in range(2)}
                    for jt in range(NTT):
                        r, jl = jt // T, jt % T
                        base = r * L
                        for hh in range(2):
                            rows = slice(64 * hh, 64 * hh + 64)
                            vcols = slice(64 * hh, 64 * hh + 128)
                            qlo, qhi = max(0, 128 * jl - 64), min(L, 128 * jl + 192)
                            w = qhi - qlo
                            c0 = qlo - (128 * jl - 64)
                            kc0 = base + 128 * jl
                            blks = [kc0 // 512] + list(range((base + qlo) // 512, (base + qhi - 1) // 512 + 1))
                            rd = [B_qkT[0][x] for x in blks]
                            st_args = (qkT[0][:, 1 + hh, kc0:kc0 + 128], qkT[0][:, 0, base + qlo:base + qhi], c0, w, rd)
                            mask_fn = (band[:, c0:c0 + w], [B_band])
                            key = (ui, hh, r)
                            bank_of = bankmaps.setdefault(key, {})
                            blocks = []
                            for blk in range(2):
                                pc0, pc1 = max(c0, 128 * blk), min(c0 + w, 128 * blk + 128)
                                if pc0 >= pc1:
                                    continue
                                qp = 128 * jl + pc0
                                blocks.append([pc0, pc1, qp // 512, qp % 512])
                            if len(blocks) == 2 and blocks[0][2] == blocks[1][2] and blocks[0][1] == blocks[1][0]:
                                blocks = [[blocks[0][0], blocks[1][1], blocks[0][2], blocks[0][3]]]
                            pv = []
                            for pc0, pc1, nb, oc in blocks:
                                first = nb not in bank_of
                                if first:
                                    bank_of[nb] = None
                                pv.append([Vt[0][:, jt, vcols], pc0, pc1, (key, nb), oc, first, False, [B_Vt[0][jt]]])
                            done_banks = []
                            if jl % 4 == 3 or jl == T - 1:
                                done_banks.append(jl // 4)
                            if jl == T - 1 and (128 * T) // 512 != jl // 4:
                                done_banks.append((128 * T) // 512)

                            def done(done_banks=done_banks, bank_of=bank_of, r=r, d=d, L=L, g=g, hh=hh):
                                for nb in done_banks:
                                    a, b = max(64, 512 * nb), min(L + 64, 512 * nb + 512)
                                    ob = bank_of[nb]
                                    dst = tokv(acc[hh], d, r, a - 64, b - 64)
                                    srcp = OVB[ob][:, a - 512 * nb:b - 512 * nb]
                                    if g == 0 and COPY_ENG == "act":
                                        act(dst, srcp, AF.Copy, reads=[B_OV[ob]], writes=[B_acc[hh]])
                                    elif g == 0:
                                        P.op("dve", lambda e, o=dst, i=srcp: e.tensor_copy(out=o, in_=i),
                                             reads=[B_OV[ob]], writes=[B_acc[hh]])
                                    else:
                                        tt(dst, srcp, dst, ALU.add, reads=[B_OV[ob], B_acc[hh]], writes=[B_acc[hh]])
                                    ov_release(ob)
                            it = mk_item_lazy(st_args, mask_fn, pv, done, ui, min(blks), bank_of)
                            need_tile = jt if (jl == T - 1) else jt + 1
                            by_bh[(need_tile // 4, hh)].append(it)
                    for b in range(4):
                        by_batch[b] = by_bh[(b, 0)] + by_bh[(b, 1)]
                    if g == 2:
                        sl = hp % 2
                        extra = []
                        if hp + 1 < 8:
                            extra.append(special(lambda hp=hp: load_wout(hp + 1), ui))
                        wi = wout_items(hp, ui)
                        for q4 in range(4):
                            for hh in range(2):
                                def nfn(hh=hh, q4=q4, sl=sl):
                                    cs = slice(q4 * 512, (q4 + 1) * 512)
                                    normalise(acc[hh][:, cs], hh, oTp[sl][:, cs], 512, [B_acc[hh]], [B_oTp[sl][q4]])
                                extra.append(special(nfn, ui))
                            if q4 >= 1:
                                extra += wi[2 * (q4 - 1):2 * q4]
                        extra += wi[6:8]
                        by_batch[3] += extra
                    units.append(dict(s=hp * 3 + g, buf=0, tok=tok_fn, evac=evac,
                                      ready=(lambda b, bb=by_batch: bb[b]),
                                      war=(lambda b, ui=ui: (lambda it: it["unit"] < ui and it["minblk"] <= b))))
            load_w(0)
            load_wout(0)
            for ui, U in enumerate(units):
                if ui + 1 < len(units):
                    load_w(ui + 1)
                run_unit(ui, U)
            drain_all()

        elif kind == 1:
            def load_bias(hp):
                sl = hp % 2
                dma("pool", biasf[sl].rearrange("p (a b) -> p a b", a=4), biasB_d[hp].rearrange("p (a b) -> p a b", a=4),
                    f"d_bias{sl}", writes=[B_bias[sl]])
                act(biasf[sl], biasf[sl], AF.Copy, reads=[B_bias[sl]], writes=[B_bias[sl]], scale=8.0)

            units = []
            for hp in range(8):
                ui = hp
                sl = hp % 2

                def tok_fn(j, c):
                    return hT[:, c, j * 128:(j + 1) * 128]

                def evac(j, pj, Bpj, qk, Bqk):
                    act(qk[:, 0:256], pj[:, 0:256], AF.Copy, reads=[Bpj], writes=[Bqk])

                by_batch = {b: [] for b in range(4)}
                for hh in range(2):
                    rows = slice(64 * hh, 64 * hh + 64)
                    vcols = slice(64 * hh, 64 * hh + 128)
                    work = []
                    for n in range(16):
                        if n < 2:
                            ms, tbl = range(0, 4), 0
                        elif n >= 14:
                            ms, tbl = range(12, 16), 0
                        else:
                            ms, tbl = range(n - 2, n + 3), 1
                        for m in ms:
                            work.append((n, m, tbl, m == ms[0], m == ms[-1]))
                    groups = [work[i:i + 4] for i in range(0, len(work), 4)]
                    ov_of = {}
                    for grp in groups:
                        state = {}

                        def st(grp=grp, state=state, rows=rows, hh=hh, sl=sl):
                            k = nxt("st", 2)
                            for ii, (n, m, tbl, fst, lst) in enumerate(grp):
                                mm(STB[k][:, ii * 128:(ii + 1) * 128], qkT[0][:, 1 + hh, m * 128:(m + 1) * 128],
                                   qkT[0][:, 0, n * 128:(n + 1) * 128], True, False,
                                   reads=[B_qkT[0][m // 4], B_qkT[0][n // 4]], writes=[B_ST[k]], skip=True)
                                u0 = 7 - 2 * (m - n)
                                mm(STB[k][:, ii * 128:(ii + 1) * 128], ident[:], bias[sl][:, hh, tbl, u0:u0 + 2, :], False, True,
                                   reads=[B_bias[sl], B_ident], writes=[B_ST[k]], skip=True)
                            wg = len(grp) * 128
                            ei = nxt("et", NE)
                            act(Et[ei][:, 0:wg], STB[k][:, 0:wg], AF.Exp, reads=[B_ST[k]], writes=[B_Et[ei]], scale=0.125)
                            state["ei"] = ei

                        def pv(grp=grp, state=state, hh=hh, sl=sl, vcols=vcols, ov_of=ov_of):
                            ei = state["ei"]
                            for ii, (n, m, tbl, fst, lst) in enumerate(grp):
                                cs = slice(ii * 128, (ii + 1) * 128)
                                nb = n // 4
                                if nb not in ov_of:
                                    ov_of[nb] = ov_alloc()
                                ob = ov_of[nb]
                                oc = (n % 4) * 128
                                mm(OVB[ob][:, oc:oc + 128], Vt[0][:, m, vcols], Et[ei][:, cs], fst, lst,
                                   reads=[B_Vt[0][m], B_Et[ei]], writes=[B_OV[ob]], skip=True)
                                if lst and n % 4 == 3:
                                    cs4 = slice(nb * 512, (nb + 1) * 512)
                                    normalise(OVB[ob][:, :], hh, oTp[sl][:, cs4], 512, [B_OV[ob]], [B_oTp[sl][nb]])
                                    ov_release(ob)
                        need = max(max(n, m) for (n, m, _, _, _) in grp)
                        mb = min(min(n, m) for (n, m, _, _, _) in grp) // 4
                        by_batch[need // 4].append(dict(st=st, pv=pv, unit=ui, minblk=mb))
                extra = []
                if hp + 1 < 8:
                    extra.append(special(lambda hp=hp: load_wout(hp + 1), ui))
                extra += wout_items(hp, ui)
                by_batch[3] += extra
                units.append(dict(s=hp, buf=0, tok=tok_fn, evac=evac, ready=(lambda b, bb=by_batch: bb[b]),
                                  war=(lambda b, ui=ui: (lambda it: it["unit"] < ui and it["minblk"] <= b))))
            load_w(0)
            load_wout(0)
            load_bias(0)
            for ui, U in enumerate(units):
                if ui + 1 < len(units):
                    load_w(ui + 1)
                    force_drain(lambda it, ui=ui: it["unit"] < ui)
                    load_bias(ui + 1)
                run_unit(ui, U)
            drain_all()

        else:
            units = []
            for kv in range(4):
                ui = kv
                bi = kv % 2

                def tok_fn(j, c):
                    return hT[:, c, j * 128:(j + 1) * 128]

                def evac(j, pj, Bpj, qk, Bqk):
                    ti = nxt("tmp", 2)
                    t0, t1, t2 = tmpf[ti]
                    si = ti
                    x = pj[:, 0:384]
                    act(t1[:], x, AF.Copy, reads=[Bpj], writes=[B_tmp[ti]])
                    tt(t0[:], t1[:], t1[:], ALU.mult, reads=[B_tmp[ti]], writes=[B_tmp[ti]], eng="pool")
                    P.op("dve", lambda e, o=ssq[si][:, 0:6], i=t0[:].rearrange("p (h e) -> p h e", e=64):
                         e.tensor_reduce(out=o, in_=i, axis=AX.X, op=ALU.add), reads=[B_tmp[ti]], writes=[B_tmp[ti]])
                    act(ssq[si][:, 0:6], ssq[si][:, 0:6], AF.Ln, reads=[B_tmp[ti]], writes=[B_tmp[ti]], scale=1.0 / 64, bias=EPS)
                    act(ssq[si][:, 0:6], ssq[si][:, 0:6], AF.Exp, reads=[B_tmp[ti]], writes=[B_tmp[ti]], scale=-0.5)
                    tt(t2[:].rearrange("p (h e) -> p h e", e=64), t1[:].rearrange("p (h e) -> p h e", e=64),
                       ssq[si][:, 0:6].unsqueeze(2).to_broadcast([128, 6, 64]), ALU.mult, reads=[B_tmp[ti]], writes=[B_tmp[ti]])
                    tt(t0[:], t2[:], cg, ALU.mult, reads=[B_tmp[ti], B_lc], writes=[B_tmp[ti]])
                    x3 = t0[:].rearrange("p (h e) -> p h e", e=32)
                    cosb = tabs[:, 0, j, :].unsqueeze(1).to_broadcast([128, 12, 32])
                    sinb = tabs[:, 1, j, :].unsqueeze(1).to_broadcast([128, 12, 32])
                    rope(x3, 6, cosb, sinb, t1, t2, qk, [B_tmp[ti], B_lc], B_tmp[ti], Bqk)

                items = []
                for g in range(4):
                    hh = g % 2
                    pi = kv * 2 + g // 2
                    sl = pi % 2
                    rows = slice(64 * hh, 64 * hh + 64)
                    vcols = slice(64 * hh, 64 * hh + 128)
                    for qc in range(4):
                        key = (ui, g, qc)
                        bank_of = bankmaps.setdefault(key, {0: None})
                        for m in range(NTT):
                            st_args = (qkT[bi][:, 2 + hh, m * 128:(m + 1) * 128], qkT[bi][:, g // 2, qc * 512:(qc + 1) * 512],
                                       0, 512, [B_qkT[bi][m // 4], B_qkT[bi][qc]])
                            pv = [[Vt[bi][:, m, vcols], 0, 512, (key, 0), 0, m == 0, m == NTT - 1, [B_Vt[bi][m]]]]
                            done = None
                            if m == NTT - 1:
                                def done(bank_of=bank_of, hh=hh, sl=sl, qc=qc):
                                    ob = bank_of[0]
                                    cs = slice(qc * 512, (qc + 1) * 512)
                                    normalise(OVB[ob][:, :], hh, oTp[sl][:, cs], 512, [B_OV[ob]], [B_oTp[sl][qc]])
                                    ov_release(ob)
                            items.append(mk_item_lazy(st_args, None, pv, done, ui, 0, bank_of))
                    if hh == 1:
                        if pi + 1 < 8:
                            items.append(special(lambda pi=pi: load_wout(pi + 1), ui))
                        items += wout_items(pi, ui)
                units.append(dict(s=kv, buf=bi, tok=tok_fn, evac=evac,
                                  ready=(lambda b, its=items: its if b == 3 else []),
                                  war=(lambda b, ui=ui: (lambda it: it["unit"] <= ui - 2))))
            load_w(0)
            load_wout(0)
            for ui, U in enumerate(units):
                if ui + 1 < len(units):
                    load_w(ui + 1)
                run_unit(ui, U)
            drain_all()

    def mlp_layer(li):
        wd = w_d[li]
        AR.reset()
        uT = AR.take((4, S), BF16)
        wup = [AR.take((NCH, 512), BF16) for _ in range(2)]
        wdn = [AR.take((4, D), BF16) for _ in range(2)]
        B_uT = [Buf() for _ in range(4)]
        B_wup = [Buf() for _ in range(2)]
        B_wdn = [Buf() for _ in range(2)]

        def load(gq):
            sl = gq % 2
            dma("pool", wup[sl], wd["w_up"][gq].rearrange("p (c n) -> p c n", n=512), f"d_wup{sl}", writes=[B_wup[sl]])
            dma("pool", wdn[sl], wd["w_dn"][gq].rearrange("p (c n) -> p c n", n=D), f"d_wdn{sl}", writes=[B_wdn[sl]])

        load(0)
        for gq in range(8):
            if gq + 1 < 8:
                load(gq + 1)
            sl = gq % 2
            for tc in range(4):
                tcs = slice(tc * 512, (tc + 1) * 512)
                for fc in range(4):
                    k = nxt("pj", 2)
                    for c in range(NCH):
                        mm(PJ[k][:], wup[sl][:, c, fc * 128:(fc + 1) * 128], hT[:, c, tcs], c == 0, c == NCH - 1,
                           reads=[B_wup[sl], B_hT[tc]], writes=[B_PJ[k]])
                    s = nxt("sq", 2)
                    act(sq[s][:], PJ[k][:], AF.Relu, reads=[B_PJ[k]], writes=[B_sq[s]])
                    tt(uT[:, fc, tcs], sq[s][:], sq[s][:], ALU.mult, reads=[B_sq[s]], writes=[B_uT[tc]])
            for tc in range(4):
                tcs = slice(tc * 512, (tc + 1) * 512)
                for m in range(NCH):
                    b4 = nxt("g4", 2 + NOV)
                    bank, Bb = (STB + OVB)[b4], (B_ST + B_OV)[b4]
                    for fc in range(4):
                        mm(bank[:], wdn[sl][:, fc, m * 128:(m + 1) * 128], uT[:, fc, tcs], fc == 0, fc == 3,
                           reads=[B_wdn[sl], B_uT[tc]], writes=[Bb])
                    tt(xT[:, m, tcs], bank[:], xT[:, m, tcs], ALU.add, reads=[Bb, B_xT[m][tc]], writes=[B_xT[m][tc]])

    for li in layer_ids:
        rmsnorm(2 * li, False)
        if dbg.get("attn", True):
            attention_layer(li)
        P.barrier()
        rmsnorm(2 * li + 1, False)
        if dbg.get("mlp", True):
            mlp_layer(li)
        P.barrier()
    if do_final:
        rmsnorm(8, True)
    outT_dv = outT_d.rearrange("(c p) t -> p c t", p=128)
    for c in range(NCH):
        dma("sp", outT_dv[:, c, :], xT[:, c, :], "d_out", reads=B_xT[c], writes=[B_out])
    P.op("sp", None, reads=[B_out])
    P.emit()
    P.close()
    return nc, P.n_ops


def _rope_tables():
    t = np.arange(S, dtype=np.int32)
    inv64 = (1.0 / (np.float32(10000.0) ** (np.arange(0, 64, 2, dtype=np.float32) / np.float32(64)))).astype(np.float32)
    inv32 = (1.0 / (np.float32(10000.0) ** (np.arange(0, 32, 2, dtype=np.float32) / np.float32(32)))).astype(np.float32)
    ang_a = t.astype(np.float32)[:, None] * inv64[None, :]
    ang_c = np.concatenate([(t // 64).astype(np.float32)[:, None] * inv32[None, :],
                            (t % 64).astype(np.float32)[:, None] * inv32[None, :]], axis=-1)
    ca, sa = np.cos(ang_a).astype(np.float32), np.sin(ang_a).astype(np.float32)
    cc, sc = np.cos(ang_c).astype(np.float32), np.sin(ang_c).astype(np.float32)
    ropeA = np.zeros((6, 128, NTT, 32), np.float32)
    for g, d in enumerate(DIL):
        L = S // d
        u = np.arange(S)
        tt = (u % L) * d + (u // L)
        for k, tab in enumerate((ca, sa)):
            ropeA[2 * g + k] = tab[tt].reshape(NTT, 128, 32).transpose(1, 0, 2)
    ropeC = np.stack([cc.reshape(NTT, 128, 32).transpose(1, 0, 2), sc.reshape(NTT, 128, 32).transpose(1, 0, 2)])
    return ropeA.reshape(6, 128, NTT * 32), ropeC.reshape(2, 128, NTT * 32)


def _bias_tables(rpb):
    p = np.arange(128)
    a, kc = p // 64, p % 64
    u = np.arange(16)
    qc = np.arange(64)
    drow = 7 - u[None, :] + a[:, None]
    cs = np.clip(qc - 8, 0, 48)
    colok = (kc[:, None] >= cs[None, :]) & (kc[:, None] < cs[None, :] + 16)
    dcol = np.clip(kc[:, None] - qc[None, :] + 15, 0, 30)
    rowok_f = (drow >= -7) & (drow <= 7)
    rowok_z = (drow >= -4) & (drow <= 3)
    di = np.clip(drow + 7, 0, 14)
    vals = rpb[:, di[:, :, None], dcol[:, None, :]]
    out = np.full((16, 128, 2, 16, 64), NEG, np.float32)
    for tbl, rok in enumerate((rowok_f, rowok_z)):
        ok = rok[:, :, None] & colok[:, None, :]
        out[:, :, tbl] = np.where(ok[None], vals, np.float32(NEG))
    out = out.reshape(8, 2, 128, 2, 16, 64).transpose(0, 2, 1, 3, 4, 5)
    return np.ascontiguousarray(out).reshape(8, 128, 2 * 2 * 16 * 64)


def _win_layout(W, kind):
    if kind == 0:
        cols = []
        for hp in range(8):
            for g in range(3):
                cols.append(np.concatenate([g * 3072 + t * 1024 + hp * 128 + np.arange(128) for t in range(3)]))
    elif kind == 1:
        cols = [np.concatenate([t * 1024 + hp * 128 + np.arange(128) for t in range(3)]) for hp in range(8)]
    else:
        cols = [np.concatenate([kv * 256 + np.arange(256), 1024 + kv * 64 + np.arange(64), 1024 + kv * 64 + np.arange(64),
                                1280 + kv * 64 + np.arange(64)]) for kv in range(4)]
    cols = np.stack(cols)
    nsl, ncols = cols.shape
    Wg = W[:, cols.reshape(-1)].reshape(NCH, 128, nsl, ncols)
    return np.ascontiguousarray(Wg.transpose(2, 1, 0, 3)).reshape(nsl, 128, NCH * ncols)


def prep_shared(inp, layer_ids):
    sh = {}
    gl = []
    for i in range(DEPTH):
        gl += [inp[f"l{i}_attn_norm"], inp[f"l{i}_mlp_norm"]]
    gl.append(inp["final_norm"])
    g = np.stack([np.asarray(x, np.float32) for x in gl])
    sh["gains"] = np.ascontiguousarray(g.reshape(9, NCH, 128).transpose(2, 0, 1)).reshape(128, 9 * NCH)
    sh["ident"] = np.eye(128, dtype=np.float32)
    kk, qq = np.arange(128)[:, None], np.arange(256)[None, :]
    sh["band"] = np.where((qq >= kk) & (qq <= kk + 128), 0.0, -30000.0).astype(np.float32)
    sh["ropeA"], sh["ropeC"] = _rope_tables()
    qn, kn = np.asarray(inp["l2_q_norm"], np.float32), np.asarray(inp["l2_k_norm"], np.float32)
    sh["cgain"] = np.ascontiguousarray(np.broadcast_to(np.concatenate([qn, qn, qn, qn, kn, kn])[None, :], (128, 384)))
    sh["biasB"] = _bias_tables(np.asarray(inp["l1_rpb"], np.float32))
    for li in layer_ids:
        kind = li % 3
        sh[f"w{li}_in"] = _win_layout(np.asarray(inp[f"l{li}_w_in"], np.float32), kind)
        sh[f"w{li}_out"] = np.ascontiguousarray(np.asarray(inp[f"l{li}_w_out"], np.float32)).reshape(8, 128, D)
        wu = np.asarray(inp[f"l{li}_w_up"], np.float32).reshape(NCH, 128, 8, 512)
        sh[f"w{li}_up"] = np.ascontiguousarray(wu.transpose(2, 1, 0, 3)).reshape(8, 128, NCH * 512)
        wdn = np.asarray(inp[f"l{li}_w_down"], np.float32).reshape(8, 4, 128, D)
        sh[f"w{li}_dn"] = np.ascontiguousarray(wdn.transpose(0, 2, 1, 3)).reshape(8, 128, 4 * D)
    return sh


_CACHE = {}


def run_layers(inp, x, layer_ids, do_final, n_cores, dbg=None):
    key = (tuple(layer_ids), do_final, str(dbg))
    if key not in _CACHE:
        _CACHE[key] = build_program(list(layer_ids), do_final, dbg)[0]
    nc = _CACHE[key]
    sh = prep_shared(inp, layer_ids)
    in_maps = []
    for b in range(n_cores):
        m = dict(sh)
        m["xT"] = np.ascontiguousarray(np.asarray(x[b], np.float32).T)
        in_maps.append(m)
    res = run_bass_kernel_spmd(nc, in_maps, core_ids=list(range(n_cores)))
    return np.stack([np.ascontiguousarray(r["outT"].T) for r in res.results])


def kernel(**inputs):
    x = np.asarray(inputs["x"], np.float32)
    out = run_layers(inputs, x, [0, 1, 2, 3], True, 8)
    return out.astype(np.float32)
```

```python
from contextlib import ExitStack
import numpy as np
import concourse.bass as bass
import concourse.mybir as mybir
from concourse.bass_utils import run_bass_kernel_spmd

F32 = mybir.dt.float32
BF16 = mybir.dt.bfloat16
AF = mybir.ActivationFunctionType
ALU = mybir.AluOpType
AX = mybir.AxisListType

S = 2048
D = 1024
NTT = 16
NCH = 8
DEPTH = 4
EPS = 1e-6
DIL = (1, 4, 16)
NEG = -1e30

ENGS = ("pe", "act", "dve", "pool", "sp")


class Buf:
    __slots__ = ("name", "w", "rs", "excl")

    def __init__(self, name="", excl=False):
        self.name = name
        self.w = None
        self.rs = []
        self.excl = excl


class Op:
    __slots__ = ("eng", "fn", "deps", "signal", "tok", "is_dma")

    def __init__(self, eng, fn, is_dma):
        self.eng = eng
        self.fn = fn
        self.deps = []
        self.signal = False
        self.tok = None
        self.is_dma = is_dma


class Prog:
    def __init__(self, nc):
        self.nc = nc
        self.ops = {e: [] for e in ENGS}
        self.stack = ExitStack()
        self.sems = {}
        self.dma_cnt = {}
        self.pending_dmas = []
        self.n_ops = 0

    def sem(self, name):
        if name not in self.sems:
            self.sems[name] = self.stack.enter_context(self.nc.semaphore(name))
        return self.sems[name]

    def sbuf(self, name, shape, dt):
        return self.stack.enter_context(self.nc.sbuf_tensor(name, list(shape), dt))

    def psum(self, name, shape, dt):
        return self.stack.enter_context(self.nc.psum_tensor(name, list(shape), dt))

    def op(self, eng, fn, reads=(), writes=(), dma_key=None):
        is_dma = dma_key is not None
        o = Op(eng, fn, is_dma)
        need = {}
        for b in reads:
            p = b.w
            if p is not None:
                need[id(p)] = (p, True)
            if b.excl:
                for r in b.rs:
                    if r.eng != eng and id(r) not in need:
                        need[id(r)] = (r, False)
        for b in writes:
            p = b.w
            if p is not None and id(p) not in need:
                need[id(p)] = (p, False)
            for r in b.rs:
                if id(r) not in need:
                    need[id(r)] = (r, False)
        for p, raw in need.values():
            if p.eng == eng and not p.is_dma and not is_dma:
                if not raw:
                    continue
                if eng == "pe":
                    continue
            p.signal = True
            o.deps.append(p)
        for b in reads:
            b.rs.append(o)
        for b in writes:
            b.w = o
            b.rs = []
        if is_dma:
            self.sem(dma_key)
            c = self.dma_cnt.get(dma_key, 0) + 16
            self.dma_cnt[dma_key] = c
            o.tok = (dma_key, c)
            o.signal = True
            self.pending_dmas.append(o)
        self.ops[eng].append(o)
        self.n_ops += 1
        return o

    def barrier(self):
        lasts = []
        for e in ENGS:
            for o in reversed(self.ops[e]):
                if not o.is_dma and o.fn is not None:
                    o.signal = True
                    lasts.append(o)
                    break
        dmas = self.pending_dmas
        self.pending_dmas = []
        for e in ENGS:
            b = Op(e, None, False)
            b.deps = [o for o in lasts if o.eng != e] + list(dmas)
            self.ops[e].append(b)

    def emit(self):
        nc = self.nc
        for e in ENGS:
            self.sem("s_" + e)
            c = 0
            for o in self.ops[e]:
                if o.is_dma or o.fn is None:
                    continue
                if o.signal:
                    c += 1
                    o.tok = ("s_" + e, c)
        sems = self.sems

        def run(e, engobj):
            waited = {}
            for o in self.ops[e]:
                for p in o.deps:
                    k, v = p.tok
                    if waited.get(k, 0) >= v:
                        continue
                    waited[k] = v
                    engobj.wait_ge(sems[k], v)
                if o.fn is None:
                    continue
                ins = o.fn(engobj)
                if o.signal:
                    ins.then_inc(sems[o.tok[0]], 16 if o.is_dma else 1)

        with nc.Block() as block:
            @block.tensor
            def _(eng):
                run("pe", eng)

            @block.scalar
            def _(eng):
                run("act", eng)

            @block.vector
            def _(eng):
                run("dve", eng)

            @block.gpsimd
            def _(eng):
                run("pool", eng)

            @block.sync
            def _(eng):
                run("sp", eng)

    def close(self):
        self.stack.close()


def tokv(ap2d, d, r, i0, i1):
    if d == 1:
        return ap2d[:, i0:i1]
    return ap2d.rearrange("p (i s) -> p i s", s=d)[:, i0:i1, r]


ARENA_BYTES = 76 * 1024


def build_program(layer_ids, do_final, dbg=None):
    dbg = dbg or {}
    nc = bass.Bass("TRN2", target_bir_lowering=False)
    P = Prog(nc)

    def din(name, shape):
        return nc.dram_tensor(name, list(shape), F32, kind="ExternalInput").ap()

    xT_d = din("xT", [D, S])
    gains_d = din("gains", [128, 9 * NCH])
    ident_d = din("ident", [128, 128])
    band_d = din("band", [128, 256])
    ropeA_d = din("ropeA", [6, 128, NTT * 32])
    ropeC_d = din("ropeC", [2, 128, NTT * 32])
    cgain_d = din("cgain", [128, 384])
    biasB_d = din("biasB", [8, 128, 2 * 2 * 16 * 64])
    w_d = {}
    for li in layer_ids:
        kind = li % 3
        nsl, ncols = ((24, 384), (8, 384), (4, 448))[kind]
        w_d[li] = dict(
            w_in=din(f"w{li}_in", [nsl, 128, NCH * ncols]),
            w_out=din(f"w{li}_out", [8, 128, D]),
            w_up=din(f"w{li}_up", [8, 128, NCH * 512]),
            w_dn=din(f"w{li}_dn", [8, 128, 4 * D]),
        )
    outT_d = nc.dram_tensor("outT", [D, S], F32, kind="ExternalOutput").ap()

    xT = P.sbuf("xT_sb", [128, NCH, S], F32)
    hT = P.sbuf("hT_sb", [128, NCH, S], BF16)
    arena = P.sbuf("arena", [128, ARENA_BYTES // 2], BF16)
    ident = P.sbuf("ident_sb", [128, 128], BF16)
    band = P.sbuf("band_sb", [128, 256], BF16)
    ones32 = P.sbuf("ones32", [128, 128], F32)
    gains = P.sbuf("gains_sb", [128, 9 * NCH], F32)
    sq = [P.sbuf(f"sq{i}", [128, 512], F32) for i in range(2)]
    rstd = P.sbuf("rstd", [128, 512], F32)
    NE, NQ, NOV = 6, 4, 3
    LATE_DELAY = dbg.get("late", 2)
    TDEL = dbg.get("tdel", 2)
    COPY_ENG = dbg.get("copy_eng", "dve")
    MASK_ENG = dbg.get("mask_eng", "pool")
    ROPE2_ENG = dbg.get("rope2_eng", "pool")
    Et = [P.sbuf(f"Et{i}", [128, 512], BF16) for i in range(NE)]
    qktm = [P.sbuf(f"qktm{i}", [128, 384], BF16) for i in range(NQ)]
    tmpf = [[P.sbuf(f"tmp{i}_{k}", [128, 384], F32) for k in range(3)] for i in range(2)]
    ssq = [P.sbuf(f"ssq{i}", [128, 8], F32) for i in range(2)]
    rc = [P.sbuf(f"rc{i}", [128, 512], F32) for i in range(2)]

    B_sq = [Buf() for _ in range(2)]
    B_rstd = Buf()
    B_Et = [Buf() for _ in range(NE)]
    B_Pt = [Buf() for _ in range(NE)]
    B_qktm = [Buf() for _ in range(NQ)]
    B_tmp = [Buf() for _ in range(2)]
    B_rc = [Buf() for _ in range(2)]
    B_xT = [[Buf() for _ in range(4)] for _ in range(NCH)]
    B_hT = [Buf() for _ in range(4)]
    B_const = Buf()
    B_gains, B_ident, B_band = Buf(), Buf(), Buf()
    B_out = Buf()

    PJ = [P.psum(f"pj{i}", [128, 512], F32) for i in range(2)]
    TR = [P.psum(f"tr{i}", [128, 1024], BF16) for i in range(1)]
    STB = [P.psum(f"st{i}", [128, 512], F32) for i in range(2)]
    OVB = [P.psum(f"ov{i}", [128, 512], F32) for i in range(NOV)]
    B_PJ = [Buf(excl=True) for _ in range(2)]
    B_TR = [Buf(excl=True) for _ in range(1)]
    B_ST = [Buf(excl=True) for _ in range(2)]
    B_OV = [Buf(excl=True) for _ in range(NOV)]
    ctr = dict(pj=0, tr=0, st=0, ov=0, et=0, qk=0, tmp=0, rc=0, sq=0, g4=0)

    def nxt(k, n):
        v = ctr[k] % n
        ctr[k] += 1
        return v

    class Arena:
        def __init__(self):
            self.off = 0

        def reset(self):
            self.off = 0

        def take(self, free_shape, dt):
            n = int(np.prod(free_shape))
            size = n * (4 if dt == F32 else 2)
            assert self.off + size <= ARENA_BYTES, (self.off, size)
            a = arena[:, self.off // 2:(self.off + size) // 2]
            self.off += (size + 63) // 64 * 64
            if dt == F32:
                a = a.bitcast(F32)
            if len(free_shape) == 2:
                a = a.rearrange("p (a b) -> p a b", a=free_shape[0])
            elif len(free_shape) == 3:
                a = a.rearrange("p (a b c) -> p a b c", a=free_shape[0], b=free_shape[1])
            return a

    AR = Arena()

    def dma(eng, out, in_, key, writes=(), reads=()):
        return P.op(eng, lambda e, o=out, i=in_: e.dma_start(out=o, in_=i), reads=reads, writes=writes, dma_key=key)

    def mm(out, lhsT, rhs, start, stop, reads, writes, skip=False):
        return P.op("pe", lambda e, o=out, l=lhsT, r=rhs, s0=start, s1=stop, sk=skip:
                    e.matmul(o, l, r, start=s0, stop=s1, skip_group_check=sk), reads=reads, writes=writes)

    def act(out, in_, func, reads, writes, scale=1.0, bias=0.0):
        return P.op("act", lambda e, o=out, i=in_, f=func, s=scale, b=bias:
                    e.activation(out=o, in_=i, func=f, bias=b, scale=s), reads=reads, writes=writes)

    def tt(out, in0, in1, op, reads, writes, eng="dve"):
        return P.op(eng, lambda e, o=out, a=in0, b=in1, p=op: e.tensor_tensor(out=o, in0=a, in1=b, op=p),
                    reads=reads, writes=writes)

    def recip(out, in_, reads, writes):
        return P.op("dve", lambda e, o=out, i=in_: e.reciprocal(out=o, in_=i), reads=reads, writes=writes)

    xT_dv = xT_d.rearrange("(c p) t -> p c t", p=128)
    for c in range(NCH):
        dma("sp", xT[:, c, :], xT_dv[:, c, :], f"d_x{c}", writes=B_xT[c])
    dma("sp", gains[:], gains_d, "d_c", writes=[B_gains])
    dma("pool", ident[:], ident_d, "d_c2", writes=[B_ident])
    dma("pool", band[:], band_d, "d_c3", writes=[B_band])
    P.op("dve", lambda e: e.memset(ones32[:], 1.0), writes=[B_const])

    def rmsnorm(gi, to_x):
        for tc in range(4):
            tcs = slice(tc * 512, (tc + 1) * 512)
            k = nxt("pj", 2)
            for c in range(NCH):
                s = nxt("sq", 2)
                act(sq[s][:], xT[:, c, tcs], AF.Square, reads=[B_xT[c][tc]], writes=[B_sq[s]])
                mm(PJ[k][:], ones32[:], sq[s][:], c == 0, c == NCH - 1, reads=[B_sq[s], B_const], writes=[B_PJ[k]])
            act(rstd[:], PJ[k][:], AF.Sqrt, reads=[B_PJ[k]], writes=[B_rstd], scale=1.0 / D, bias=EPS)
            recip(rstd[:], rstd[:], reads=[B_rstd], writes=[B_rstd])
            for c in range(NCH):
                dst = xT[:, c, tcs] if to_x else hT[:, c, tcs]
                wr = [B_xT[c][tc]] if to_x else [B_hT[tc]]
                P.op("dve", lambda e, o=dst, a=xT[:, c, tcs], g=gains[:, gi * NCH + c:gi * NCH + c + 1]:
                     e.scalar_tensor_tensor(out=o, in0=a, scalar=g, in1=rstd[:], op0=ALU.mult, op1=ALU.mult),
                     reads=[B_xT[c][tc], B_rstd, B_gains], writes=wr)

    def attention_layer(li):
        kind = li % 3
        wd = w_d[li]
        nsl, ncols = ((24, 384), (8, 384), (4, 448))[kind]
        nT = 3 if kind == 2 else 2
        tb = 2 if kind == 2 else 4
        nqk = 384 if kind == 2 else 256
        nbuf = 2 if kind == 2 else 1
        AR.reset()
        wsl = [AR.take((NCH, ncols), BF16) for _ in range(2)]
        nS = nT + 1
        qkT = [AR.take((nS, S), BF16) for _ in range(nbuf)]
        Vt = [AR.take((NTT, 192), BF16) for _ in range(nbuf)]
        oTp = [AR.take((S,), BF16) for _ in range(2)]
        woutp = [AR.take((D,), BF16) for _ in range(2)]
        B_wsl = [Buf() for _ in range(2)]
        B_qkT = [[Buf() for _ in range(4)] for _ in range(nbuf)]
        B_Vt = [[Buf() for _ in range(NTT)] for _ in range(nbuf)]
        B_oTp = [[Buf() for _ in range(4)] for _ in range(2)]
        B_wout = [Buf() for _ in range(2)]
        B_lc = Buf()
        if kind == 0:
            acc = [AR.take((S,), F32) for _ in range(2)]
            B_acc = [Buf() for _ in range(2)]
            tabsf = AR.take((6 * NTT * 32,), F32)
            tabs = tabsf.rearrange("p (a b c) -> p a b c", a=6, b=NTT)
            for i in range(6):
                dma("sp", tabsf[:, i * 512:(i + 1) * 512], ropeA_d[i], "d_lc", writes=[B_lc])
        elif kind == 1:
            biasf = [AR.take((4096,), BF16) for _ in range(2)]
            bias = [b.rearrange("p (h t u e) -> p h t u e", h=2, t=2, u=16) for b in biasf]
            B_bias = [Buf() for _ in range(2)]
        else:
            tabsf = AR.take((2 * NTT * 32,), F32)
            tabs = tabsf.rearrange("p (a b c) -> p a b c", a=2, b=NTT)
            cg = AR.take((384,), F32)
            for i in range(2):
                dma("sp", tabsf[:, i * 512:(i + 1) * 512], ropeC_d[i], "d_lc", writes=[B_lc])
            dma("sp", cg, cgain_d, "d_lc", writes=[B_lc])
        for bi in range(nbuf):
            P.op("dve", lambda e, v=Vt[bi]: e.memset(v[:, :, 64:128], 1.0), writes=B_Vt[bi])
            P.op("pool", lambda e, q=qkT[bi]: e.memset(q[64:128, nT - 1, :], 0.0), writes=B_qkT[bi])
            P.op("pool", lambda e, q=qkT[bi]: e.memset(q[0:64, nT, :], 0.0), writes=B_qkT[bi])

        def load_w(s):
            sl = s % 2
            dma("pool", wsl[sl], wd["w_in"][s].rearrange("p (c n) -> p c n", n=ncols), f"d_wsl{sl}", writes=[B_wsl[sl]])

        def load_wout(pi):
            sl = pi % 2
            dma("pool", woutp[sl], wd["w_out"][pi], f"d_wo{sl}", writes=[B_wout[sl]])

        pend = []

        late = []
        if kind == 2:
            st_pool = [(STB[0], B_ST[0]), (STB[1], B_ST[1]), (PJ[0], B_PJ[0]), (PJ[1], B_PJ[1])]
        else:
            st_pool = [(STB[0], B_ST[0]), (STB[1], B_ST[1])]
        infl = []
        DEPTH_IT = dbg.get("depth", 4 if kind == 2 else 2)

        def step():
            it = pend.pop(0)
            it["st"]()
            infl.append(it)
            if len(infl) > DEPTH_IT:
                infl.pop(0)["pv"]()

        def force_drain(pred):
            lastm = -1
            for i_, ent in enumerate(late):
                if any(pred(it) for it in ent[1]):
                    lastm = i_
            for _ in range(lastm + 1):
                pend.extend(late.pop(0)[1])
            while any(pred(it) for it in pend):
                step()
            while infl and any(pred(it) for it in infl):
                infl.pop(0)["pv"]()

        def drain_all():
            while late:
                pend.extend(late.pop(0)[1])
            while pend:
                step()
            while infl:
                infl.pop(0)["pv"]()

        def special(fn, unit):
            return dict(st=lambda: None, pv=fn, unit=unit, minblk=0)

        def mk_item(st_args, mask_fn, pv_list, done_fn, unit, minblk):
            state = {}

            def st():
                k = nxt("st", 2)
                lhsT, rhs, c0, w, rd = st_args
                mm(STB[k][:, c0:c0 + w], lhsT, rhs, True, True, reads=rd, writes=[B_ST[k]])
                ei = nxt("et", NE)
                act(Et[ei][:, c0:c0 + w], STB[k][:, c0:c0 + w], AF.Exp, reads=[B_ST[k]], writes=[B_Et[ei]], scale=0.125)
                src, Bsrc = Et[ei], B_Et[ei]
                if mask_fn is not None:
                    mask_fn(Et[ei], Pt[ei], B_Et[ei], B_Pt[ei])
                    src, Bsrc = Pt[ei], B_Pt[ei]
                state["src"] = (src, Bsrc)

            def pv():
                src, Bsrc = state["src"]
                for (lhsT, pc0, pc1, ob, oc, start, stop, rd) in pv_list:
                    mm(OVB[ob][:, oc:oc + (pc1 - pc0)], lhsT, src[:, pc0:pc1], start, stop,
                       reads=rd + [Bsrc], writes=[B_OV[ob]], skip=True)
                if done_fn is not None:
                    done_fn()
            return dict(st=st, pv=pv, unit=unit, minblk=minblk)

        bankmaps = {}
        ov_open = set()
        ov_lru = list(range(NOV))

        def ov_alloc():
            for p in ov_lru:
                if p not in ov_open:
                    ov_lru.remove(p)
                    ov_lru.append(p)
                    ov_open.add(p)
                    return p
            raise AssertionError("no free OV bank")

        def ov_release(p):
            ov_open.discard(p)

        def mk_item_lazy(st_args, mask_fn, pv_list, done_fn, unit, minblk, bank_of):
            state = {}

            def st():
                stb, Bstb = st_pool[nxt("st", len(st_pool))]
                lhsT, rhs, c0, w, rd = st_args
                mm(stb[:, c0:c0 + w], lhsT, rhs, True, mask_fn is None, reads=rd, writes=[Bstb])
                if mask_fn is not None:
                    mrhs, mrd = mask_fn
                    mm(stb[:, c0:c0 + w], ident[:], mrhs, False, True, reads=mrd + [B_ident], writes=[Bstb])
                ei = nxt("et", NE)
                act(Et[ei][:, c0:c0 + w], stb[:, c0:c0 + w], AF.Exp, reads=[Bstb], writes=[B_Et[ei]], scale=0.125)
                state["src"] = (Et[ei], B_Et[ei])

            def pv():
                src, Bsrc = state["src"]
                for (lhsT, pc0, pc1, lb, oc, start, stop, rd) in pv_list:
                    nb = lb[1]
                    if start:
                        bank_of[nb] = ov_alloc()
                    ob = bank_of[nb]
                    mm(OVB[ob][:, oc:oc + (pc1 - pc0)], lhsT, src[:, pc0:pc1], start, stop,
                       reads=rd + [Bsrc], writes=[B_OV[ob]], skip=True)
                if done_fn is not None:
                    done_fn()
            return dict(st=st, pv=pv, unit=unit, minblk=minblk)

        def normalise(src, hh, dst, w, reads, writes):
            ri = nxt("rc", 2)
            orow = slice(64 * hh, 64 * hh + 64)
            drow = slice(64 * (1 - hh), 64 * (1 - hh) + 64)
            if kind == 2:
                recip(rc[ri][orow, 0:w], src[drow, :], reads=reads, writes=[B_rc[ri]])
            else:
                act(rc[ri][orow, 0:w], src[drow, :], AF.Ln, reads=reads, writes=[B_rc[ri]])
                act(rc[ri][orow, 0:w], rc[ri][orow, 0:w], AF.Exp, reads=[B_rc[ri]], writes=[B_rc[ri]], scale=-1.0)
            tt(dst[orow, :], src[orow, :], rc[ri][orow, 0:w], ALU.mult, reads=reads + [B_rc[ri]], writes=writes)

        def wout_items(pi, unit):
            sl = pi % 2
            res = []
            for tc in range(4):
                for mh in range(2):
                    def fn(tc=tc, mh=mh):
                        for m in range(mh * 4, mh * 4 + 4):
                            b4 = nxt("g4", 2 + NOV)
                            while b4 >= 2 and (b4 - 2) in ov_open:
                                b4 = nxt("g4", 2 + NOV)
                            bank, Bb = (STB + OVB)[b4], (B_ST + B_OV)[b4]
                            mm(bank[:], woutp[sl][:, m * 128:(m + 1) * 128], oTp[sl][:, tc * 512:(tc + 1) * 512], True, True,
                               reads=[B_wout[sl], B_oTp[sl][tc]], writes=[Bb])
                            tt(xT[:, m, tc * 512:(tc + 1) * 512], bank[:], xT[:, m, tc * 512:(tc + 1) * 512], ALU.add,
                               reads=[Bb, B_xT[m][tc]], writes=[B_xT[m][tc]])
                    res.append(special(fn, unit))
            return res

        def rope(x3, nh, cosb, sinb, t1, t2, dst, reads, B_t, B_dst):
            n = 2 * nh * 32
            t1v = t1[:, 0:n].rearrange("p (h e) -> p h e", e=32)
            t2v = t2[:, 0:n].rearrange("p (h e) -> p h e", e=32)
            tt(t1v, x3, cosb, ALU.mult, reads=reads, writes=[B_t])
            tt(t2v, x3, sinb, ALU.mult, reads=reads, writes=[B_t])
            t1q = t1[:, 0:n].rearrange("p (h t e) -> p h t e", t=2, e=32)
            t2q = t2[:, 0:n].rearrange("p (h t e) -> p h t e", t=2, e=32)
            dq = dst[:, 0:n].rearrange("p (h t e) -> p h t e", t=2, e=32)
            tt(dq[:, :, 0, :], t1q[:, :, 0, :], t2q[:, :, 1, :], ALU.subtract, reads=[B_t], writes=[B_dst], eng=ROPE2_ENG)
            tt(dq[:, :, 1, :], t1q[:, :, 1, :], t2q[:, :, 0, :], ALU.add, reads=[B_t], writes=[B_dst], eng=ROPE2_ENG)

        def run_unit(ui, U):
            sl = U["s"] % 2
            bi = U["buf"]
            st = dict(trk=None, qis={})

            def transposes(j):
                qi = st["qis"].pop(j)
                jj = j % tb
                if jj == 0:
                    st["trk"] = nxt("tr", 1)
                trk = st["trk"]
                trv = TR[trk][:, 0:nT * tb * 128].rearrange("p (n x) -> p n x", n=nT)
                for blk in range(nT):
                    P.op("pe", lambda e, o=trv[:, blk, jj * 128:(jj + 1) * 128], i=qktm[qi][:, blk * 128:(blk + 1) * 128]:
                         e.transpose(o, i, ident[:]), reads=[B_qktm[qi], B_ident], writes=[B_TR[trk]])
                if jj == tb - 1:
                    jb0 = (j - tb + 1) * 128
                    cs_ = slice(jb0, jb0 + tb * 128)
                    Bq = [B_qkT[bi][jb0 // 512]]
                    for (o_, i_) in ((qkT[bi][:, 0:nT - 1, cs_], trv[:, 0:nT - 1, :]),
                                     (qkT[bi][0:64, nT - 1, cs_], trv[0:64, nT - 1, :]),
                                     (qkT[bi][64:128, nT, cs_], trv[64:128, nT - 1, :])):
                        if COPY_ENG == "act":
                            act(o_, i_, AF.Copy, reads=[B_TR[trk]], writes=Bq)
                        else:
                            P.op("dve", lambda e, o=o_, i=i_: e.tensor_copy(out=o, in_=i), reads=[B_TR[trk]], writes=Bq)
                    if (j + 1) % 4 == 0:
                        late.append([LATE_DELAY, U["ready"](j // 4)])

            def tile(j):
                k = nxt("pj", 2)
                for c in range(NCH):
                    mm(PJ[k][:, 0:ncols], U["tok"](j, c), wsl[sl][:, c, :], c == 0, c == NCH - 1,
                       reads=B_hT + [B_wsl[sl]], writes=[B_PJ[k]])
                vdst = Vt[bi][:, j, :].rearrange("p (b e) -> p b e", e=64)[:, 0:3:2, :]
                if kind != 2:
                    vsrc = PJ[k][:, nqk:nqk + 128].rearrange("p (b e) -> p b e", e=64)
                else:
                    vsrc = PJ[k][:, 384:448].unsqueeze(1).to_broadcast([128, 2, 64])
                act(vdst, vsrc, AF.Copy, reads=[B_PJ[k]], writes=[B_Vt[bi][j]])
                qi = nxt("qk", NQ)
                st["qis"][j] = qi
                U["evac"](j, PJ[k], B_PJ[k], qktm[qi], B_qktm[qi])
                if j >= TDEL:
                    transposes(j - TDEL)

            for j in range(NTT):
                for ent in late:
                    ent[0] -= 1
                while late and late[0][0] <= 0:
                    pend.extend(late.pop(0)[1])
                if j % 4 == 0:
                    force_drain(U["war"](j // 4))
                kk = (3 if len(pend) > 20 else 2) if kind != 2 else 6
                for _ in range(min(kk, len(pend))):
                    step()
                tile(j)
            for jj_ in range(NTT - TDEL, NTT):
                for _ in range(min(2, len(pend))):
                    step()
                transposes(jj_)

        if kind == 0:
            units = []
            for hp in range(dbg.get("units", 8)):
                for g in range(3):
                    ui = len(units)
                    d = DIL[g]
                    L = S // d
                    T = L // 128

                    def tok_fn(j, c, d=d, L=L):
                        u0 = j * 128
                        r, i0 = u0 // L, u0 % L
                        return tokv(hT[:, c, :], d, r, i0, i0 + 128)

                    def evac(j, pj, Bpj, qk, Bqk, g=g):
                        ti = nxt("tmp", 2)
                        x3 = pj[:, 0:256].rearrange("p (h e) -> p h e", e=32)
                        cosb = tabs[:, 2 * g, j, :].unsqueeze(1).to_broadcast([128, 8, 32])
                        sinb = tabs[:, 2 * g + 1, j, :].unsqueeze(1).to_broadcast([128, 8, 32])
                        rope(x3, 4, cosb, sinb, tmpf[ti][0], tmpf[ti][1], qk, [Bpj, B_lc], B_tmp[ti], Bqk)

                    by_batch = {b: [] for b in range(4)}
                    by_bh = {(b, h): [] for b in range(4) for h in range(2)}
                    for jt in range(NTT):
                        r, jl = jt // T, jt % T
                        base = r * L
                        for hh in range(2):
                            rows = slice(64 * hh, 64 * hh + 64)
                            vcols = slice(64 * hh, 64 * hh + 128)
                            qlo, qhi = max(0, 128 * jl - 64), min(L, 128 * jl + 192)
                            w = qhi - qlo
                            c0 = qlo - (128 * jl - 64)
                            kc0 = base + 128 * jl
                            blks = [kc0 // 512] + list(range((base + qlo) // 512, (base + qhi - 1) // 512 + 1))
                            rd = [B_qkT[0][x] for x in blks]
                            st_args = (qkT[0][:, 1 + hh, kc0:kc0 + 128], qkT[0][:, 0, base + qlo:base + qhi], c0, w, rd)
                            mask_fn = (band[:, c0:c0 + w], [B_band])
                            key = (ui, hh, r)
                            bank_of = bankmaps.setdefault(key, {})
                            blocks = []
                            for blk in range(2):
                                pc0, pc1 = max(c0, 128 * blk), min(c0 + w, 128 * blk + 128)
                                if pc0 >= pc1:
                                    continue
                                qp = 128 * jl + pc0
                                blocks.append([pc0, pc1, qp // 512, qp % 512])
                            if len(blocks) == 2 and blocks[0][2] == blocks[1][2] and blocks[0][1] == blocks[1][0]:
                                blocks = [[blocks[0][0], blocks[1][1], blocks[0][2], blocks[0][3]]]
                            pv = []
                            for pc0, pc1, nb, oc in blocks:
                                first = nb not in bank_of
                                if first:
                                    bank_of[nb] = None
                                pv.append([Vt[0][:, jt, vcols], pc0, pc1, (key, nb), oc, first, False, [B_Vt[0][jt]]])
                            done_banks = []
                            if jl % 4 == 3 or jl == T - 1:
                                done_banks.append(jl // 4)
                            if jl == T - 1 and (128 * T) // 512 != jl // 4:
                                done_banks.append((128 * T) // 512)

                            def done(done_banks=done_banks, bank_of=bank_of, r=r, d=d, L=L, g=g, hh=hh):
                                for nb in done_banks:
                                    a, b = max(64, 512 * nb), min(L + 64, 512 * nb + 512)
                                    ob = bank_of[nb]
                                    dst = tokv(acc[hh], d, r, a - 64, b - 64)
                                    srcp = OVB[ob][:, a - 512 * nb:b - 512 * nb]
                                    if g == 0 and COPY_ENG == "act":
                                        act(dst, srcp, AF.Copy, reads=[B_OV[ob]], writes=[B_acc[hh]])
                                    elif g == 0:
                                        P.op("dve", lambda e, o=dst, i=srcp: e.tensor_copy(out=o, in_=i),
                                             reads=[B_OV[ob]], writes=[B_acc[hh]])
                                    else:
                                        tt(dst, srcp, dst, ALU.add, reads=[B_OV[ob], B_acc[hh]], writes=[B_acc[hh]])
                                    ov_release(ob)
                            it = mk_item_lazy(st_args, mask_fn, pv, done, ui, min(blks), bank_of)
                            need_tile = jt if (jl == T - 1) else jt + 1
                            by_bh[(need_tile // 4, hh)].append(it)
                    for b in range(4):
                        by_batch[b] = by_bh[(b, 0)] + by_bh[(b, 1)]
                    if g == 2:
                        sl = hp % 2
                        extra = []
                        if hp + 1 < 8:
                            extra.append(special(lambda hp=hp: load_wout(hp + 1), ui))
                        wi = wout_items(hp, ui)
                        for q4 in range(4):
                            for hh in range(2):
                                def nfn(hh=hh, q4=q4, sl=sl):
                                    cs = slice(q4 * 512, (q4 + 1) * 512)
                                    normalise(acc[hh][:, cs], hh, oTp[sl][:, cs], 512, [B_acc[hh]], [B_oTp[sl][q4]])
                                extra.append(special(nfn, ui))
                            if q4 >= 1:
                                extra += wi[2 * (q4 - 1):2 * q4]
                        extra += wi[6:8]
                        by_batch[3] += extra
                    units.append(dict(s=hp * 3 + g, buf=0, tok=tok_fn, evac=evac,
                                      ready=(lambda b, bb=by_batch: bb[b]),
                                      war=(lambda b, ui=ui: (lambda it: it["unit"] < ui and it["minblk"] <= b))))
            load_w(0)
            load_wout(0)
            for ui, U in enumerate(units):
                if ui + 1 < len(units):
                    load_w(ui + 1)
                run_unit(ui, U)
            drain_all()

        elif kind == 1:
            def load_bias(hp):
                sl = hp % 2
                dma("pool", biasf[sl].rearrange("p (a b) -> p a b", a=4), biasB_d[hp].rearrange("p (a b) -> p a b", a=4),
                    f"d_bias{sl}", writes=[B_bias[sl]])
                act(biasf[sl], biasf[sl], AF.Copy, reads=[B_bias[sl]], writes=[B_bias[sl]], scale=8.0)

            units = []
            for hp in range(8):
                ui = hp
                sl = hp % 2

                def tok_fn(j, c):
                    return hT[:, c, j * 128:(j + 1) * 128]

                def evac(j, pj, Bpj, qk, Bqk):
                    act(qk[:, 0:256], pj[:, 0:256], AF.Copy, reads=[Bpj], writes=[Bqk])

                by_batch = {b: [] for b in range(4)}
                for hh in range(2):
                    rows = slice(64 * hh, 64 * hh + 64)
                    vcols = slice(64 * hh, 64 * hh + 128)
                    work = []
                    for n in range(16):
                        if n < 2:
                            ms, tbl = range(0, 4), 0
                        elif n >= 14:
                            ms, tbl = range(12, 16), 0
                        else:
                            ms, tbl = range(n - 2, n + 3), 1
                        for m in ms:
                            work.append((n, m, tbl, m == ms[0], m == ms[-1]))
                    groups = [work[i:i + 4] for i in range(0, len(work), 4)]
                    ov_of = {}
                    for grp in groups:
                        state = {}

                        def st(grp=grp, state=state, rows=rows, hh=hh, sl=sl):
                            k = nxt("st", 2)
                            for ii, (n, m, tbl, fst, lst) in enumerate(grp):
                                mm(STB[k][:, ii * 128:(ii + 1) * 128], qkT[0][:, 1 + hh, m * 128:(m + 1) * 128],
                                   qkT[0][:, 0, n * 128:(n + 1) * 128], True, False,
                                   reads=[B_qkT[0][m // 4], B_qkT[0][n // 4]], writes=[B_ST[k]], skip=True)
                                u0 = 7 - 2 * (m - n)
                                mm(STB[k][:, ii * 128:(ii + 1) * 128], ident[:], bias[sl][:, hh, tbl, u0:u0 + 2, :], False, True,
                                   reads=[B_bias[sl], B_ident], writes=[B_ST[k]], skip=True)
                            wg = len(grp) * 128
                            ei = nxt("et", NE)
                            act(Et[ei][:, 0:wg], STB[k][:, 0:wg], AF.Exp, reads=[B_ST[k]], writes=[B_Et[ei]], scale=0.125)
                            state["ei"] = ei

                        def pv(grp=grp, state=state, hh=hh, sl=sl, vcols=vcols, ov_of=ov_of):
                            ei = state["ei"]
                            for ii, (n, m, tbl, fst, lst) in enumerate(grp):
                                cs = slice(ii * 128, (ii + 1) * 128)
                                nb = n // 4
                                if nb not in ov_of:
                                    ov_of[nb] = ov_alloc()
                                ob = ov_of[nb]
                                oc = (n % 4) * 128
                                mm(OVB[ob][:, oc:oc + 128], Vt[0][:, m, vcols], Et[ei][:, cs], fst, lst,
                                   reads=[B_Vt[0][m], B_Et[ei]], writes=[B_OV[ob]], skip=True)
                                if lst and n % 4 == 3:
                                    cs4 = slice(nb * 512, (nb + 1) * 512)
                                    normalise(OVB[ob][:, :], hh, oTp[sl][:, cs4], 512, [B_OV[ob]], [B_oTp[sl][nb]])
                                    ov_release(ob)
                        need = max(max(n, m) for (n, m, _, _, _) in grp)
                        mb = min(min(n, m) for (n, m, _, _, _) in grp) // 4
                        by_batch[need // 4].append(dict(st=st, pv=pv, unit=ui, minblk=mb))
                extra = []
                if hp + 1 < 8:
                    extra.append(special(lambda hp=hp: load_wout(hp + 1), ui))
                extra += wout_items(hp, ui)
                by_batch[3] += extra
                units.append(dict(s=hp, buf=0, tok=tok_fn, evac=evac, ready=(lambda b, bb=by_batch: bb[b]),
                                  war=(lambda b, ui=ui: (lambda it: it["unit"] < ui and it["minblk"] <= b))))
            load_w(0)
            load_wout(0)
            load_bias(0)
            for ui, U in enumerate(units):
                if ui + 1 < len(units):
                    load_w(ui + 1)
                    force_drain(lambda it, ui=ui: it["unit"] < ui)
                    load_bias(ui + 1)
                run_unit(ui, U)
            drain_all()

        else:
            units = []
            for kv in range(4):
                ui = kv
                bi = kv % 2

                def tok_fn(j, c):
                    return hT[:, c, j * 128:(j + 1) * 128]

                def evac(j, pj, Bpj, qk, Bqk):
                    ti = nxt("tmp", 2)
                    t0, t1, t2 = tmpf[ti]
                    si = ti
                    x = pj[:, 0:384]
                    act(t1[:], x, AF.Copy, reads=[Bpj], writes=[B_tmp[ti]])
                    tt(t0[:], t1[:], t1[:], ALU.mult, reads=[B_tmp[ti]], writes=[B_tmp[ti]], eng="pool")
                    P.op("dve", lambda e, o=ssq[si][:, 0:6], i=t0[:].rearrange("p (h e) -> p h e", e=64):
                         e.tensor_reduce(out=o, in_=i, axis=AX.X, op=ALU.add), reads=[B_tmp[ti]], writes=[B_tmp[ti]])
                    act(ssq[si][:, 0:6], ssq[si][:, 0:6], AF.Ln, reads=[B_tmp[ti]], writes=[B_tmp[ti]], scale=1.0 / 64, bias=EPS)
                    act(ssq[si][:, 0:6], ssq[si][:, 0:6], AF.Exp, reads=[B_tmp[ti]], writes=[B_tmp[ti]], scale=-0.5)
                    tt(t2[:].rearrange("p (h e) -> p h e", e=64), t1[:].rearrange("p (h e) -> p h e", e=64),
                       ssq[si][:, 0:6].unsqueeze(2).to_broadcast([128, 6, 64]), ALU.mult, reads=[B_tmp[ti]], writes=[B_tmp[ti]])
                    tt(t0[:], t2[:], cg, ALU.mult, reads=[B_tmp[ti], B_lc], writes=[B_tmp[ti]])
                    x3 = t0[:].rearrange("p (h e) -> p h e", e=32)
                    cosb = tabs[:, 0, j, :].unsqueeze(1).to_broadcast([128, 12, 32])
                    sinb = tabs[:, 1, j, :].unsqueeze(1).to_broadcast([128, 12, 32])
                    rope(x3, 6, cosb, sinb, t1, t2, qk, [B_tmp[ti], B_lc], B_tmp[ti], Bqk)

                items = []
                for g in range(4):
                    hh = g % 2
                    pi = kv * 2 + g // 2
                    sl = pi % 2
                    rows = slice(64 * hh, 64 * hh + 64)
                    vcols = slice(64 * hh, 64 * hh + 128)
                    for qc in range(4):
                        key = (ui, g, qc)
                        bank_of = bankmaps.setdefault(key, {0: None})
                        for m in range(NTT):
                            st_args = (qkT[bi][:, 2 + hh, m * 128:(m + 1) * 128], qkT[bi][:, g // 2, qc * 512:(qc + 1) * 512],
                                       0, 512, [B_qkT[bi][m // 4], B_qkT[bi][qc]])
                            pv = [[Vt[bi][:, m, vcols], 0, 512, (key, 0), 0, m == 0, m == NTT - 1, [B_Vt[bi][m]]]]
                            done = None
                            if m == NTT - 1:
                                def done(bank_of=bank_of, hh=hh, sl=sl, qc=qc):
                                    ob = bank_of[0]
                                    cs = slice(qc * 512, (qc + 1) * 512)
                                    normalise(OVB[ob][:, :], hh, oTp[sl][:, cs], 512, [B_OV[ob]], [B_oTp[sl][qc]])
                                    ov_release(ob)
                            items.append(mk_item_lazy(st_args, None, pv, done, ui, 0, bank_of))
                    if hh == 1:
                        if pi + 1 < 8:
                            items.append(special(lambda pi=pi: load_wout(pi + 1), ui))
                        items += wout_items(pi, ui)
                units.append(dict(s=kv, buf=bi, tok=tok_fn, evac=evac,
                                  ready=(lambda b, its=items: its if b == 3 else []),
                                  war=(lambda b, ui=ui: (lambda it: it["unit"] <= ui - 2))))
            load_w(0)
            load_wout(0)
            for ui, U in enumerate(units):
                if ui + 1 < len(units):
                    load_w(ui + 1)
                run_unit(ui, U)
            drain_all()

    def mlp_layer(li):
        wd = w_d[li]
        AR.reset()
        uT = AR.take((4, S), BF16)
        wup = [AR.take((NCH, 512), BF16) for _ in range(2)]
        wdn = [AR.take((4, D), BF16) for _ in range(2)]
        B_uT = [Buf() for _ in range(4)]
        B_wup = [Buf() for _ in range(2)]
        B_wdn = [Buf() for _ in range(2)]

        def load(gq):
            sl = gq % 2
            dma("pool", wup[sl], wd["w_up"][gq].rearrange("p (c n) -> p c n", n=512), f"d_wup{sl}", writes=[B_wup[sl]])
            dma("pool", wdn[sl], wd["w_dn"][gq].rearrange("p (c n) -> p c n", n=D), f"d_wdn{sl}", writes=[B_wdn[sl]])

        load(0)
        for gq in range(8):
            if gq + 1 < 8:
                load(gq + 1)
            sl = gq % 2
            for tc in range(4):
                tcs = slice(tc * 512, (tc + 1) * 512)
                for fc in range(4):
                    k = nxt("pj", 2)
                    for c in range(NCH):
                        mm(PJ[k][:], wup[sl][:, c, fc * 128:(fc + 1) * 128], hT[:, c, tcs], c == 0, c == NCH - 1,
                           reads=[B_wup[sl], B_hT[tc]], writes=[B_PJ[k]])
                    s = nxt("sq", 2)
                    act(sq[s][:], PJ[k][:], AF.Relu, reads=[B_PJ[k]], writes=[B_sq[s]])
                    tt(uT[:, fc, tcs], sq[s][:], sq[s][:], ALU.mult, reads=[B_sq[s]], writes=[B_uT[tc]])
            for tc in range(4):
                tcs = slice(tc * 512, (tc + 1) * 512)
                for m in range(NCH):
                    b4 = nxt("g4", 2 + NOV)
                    bank, Bb = (STB + OVB)[b4], (B_ST + B_OV)[b4]
                    for fc in range(4):
                        mm(bank[:], wdn[sl][:, fc, m * 128:(m + 1) * 128], uT[:, fc, tcs], fc == 0, fc == 3,
                           reads=[B_wdn[sl], B_uT[tc]], writes=[Bb])
                    tt(xT[:, m, tcs], bank[:], xT[:, m, tcs], ALU.add, reads=[Bb, B_xT[m][tc]], writes=[B_xT[m][tc]])

    for li in layer_ids:
        rmsnorm(2 * li, False)
        if dbg.get("attn", True):
            attention_layer(li)
        P.barrier()
        rmsnorm(2 * li + 1, False)
        if dbg.get("mlp", True):
            mlp_layer(li)
        P.barrier()
    if do_final:
        rmsnorm(8, True)
    outT_dv = outT_d.rearrange("(c p) t -> p c t", p=128)
    for c in range(NCH):
        dma("sp", outT_dv[:, c, :], xT[:, c, :], "d_out", reads=B_xT[c], writes=[B_out])
    P.op("sp", None, reads=[B_out])
    P.emit()
    P.close()
    return nc, P.n_ops


def _rope_tables():
    t = np.arange(S, dtype=np.int32)
    inv64 = (1.0 / (np.float32(10000.0) ** (np.arange(0, 64, 2, dtype=np.float32) / np.float32(64)))).astype(np.float32)
    inv32 = (1.0 / (np.float32(10000.0) ** (np.arange(0, 32, 2, dtype=np.float32) / np.float32(32)))).astype(np.float32)
    ang_a = t.astype(np.float32)[:, None] * inv64[None, :]
    ang_c = np.concatenate([(t // 64).astype(np.float32)[:, None] * inv32[None, :],
                            (t % 64).astype(np.float32)[:, None] * inv32[None, :]], axis=-1)
    ca, sa = np.cos(ang_a).astype(np.float32), np.sin(ang_a).astype(np.float32)
    cc, sc = np.cos(ang_c).astype(np.float32), np.sin(ang_c).astype(np.float32)
    ropeA = np.zeros((6, 128, NTT, 32), np.float32)
    for g, d in enumerate(DIL):
        L = S // d
        u = np.arange(S)
        tt = (u % L) * d + (u // L)
        for k, tab in enumerate((ca, sa)):
            ropeA[2 * g + k] = tab[tt].reshape(NTT, 128, 32).transpose(1, 0, 2)
    ropeC = np.stack([cc.reshape(NTT, 128, 32).transpose(1, 0, 2), sc.reshape(NTT, 128, 32).transpose(1, 0, 2)])
    return ropeA.reshape(6, 128, NTT * 32), ropeC.reshape(2, 128, NTT * 32)


def _bias_tables(rpb):
    p = np.arange(128)
    a, kc = p // 64, p % 64
    u = np.arange(16)
    qc = np.arange(64)
    drow = 7 - u[None, :] + a[:, None]
    cs = np.clip(qc - 8, 0, 48)
    colok = (kc[:, None] >= cs[None, :]) & (kc[:, None] < cs[None, :] + 16)
    dcol = np.clip(kc[:, None] - qc[None, :] + 15, 0, 30)
    rowok_f = (drow >= -7) & (drow <= 7)
    rowok_z = (drow >= -4) & (drow <= 3)
    di = np.clip(drow + 7, 0, 14)
    vals = rpb[:, di[:, :, None], dcol[:, None, :]]
    out = np.full((16, 128, 2, 16, 64), NEG, np.float32)
    for tbl, rok in enumerate((rowok_f, rowok_z)):
        ok = rok[:, :, None] & colok[:, None, :]
        out[:, :, tbl] = np.where(ok[None], vals, np.float32(NEG))
    out = out.reshape(8, 2, 128, 2, 16, 64).transpose(0, 2, 1, 3, 4, 5)
    return np.ascontiguousarray(out).reshape(8, 128, 2 * 2 * 16 * 64)


def _win_layout(W, kind):
    if kind == 0:
        cols = []
        for hp in range(8):
            for g in range(3):
                cols.append(np.concatenate([g * 3072 + t * 1024 + hp * 128 + np.arange(128) for t in range(3)]))
    elif kind == 1:
        cols = [np.concatenate([t * 1024 + hp * 128 + np.arange(128) for t in range(3)]) for hp in range(8)]
    else:
        cols = [np.concatenate([kv * 256 + np.arange(256), 1024 + kv * 64 + np.arange(64), 1024 + kv * 64 + np.arange(64),
                                1280 + kv * 64 + np.arange(64)]) for kv in range(4)]
    cols = np.stack(cols)
    nsl, ncols = cols.shape
    Wg = W[:, cols.reshape(-1)].reshape(NCH, 128, nsl, ncols)
    return np.ascontiguousarray(Wg.transpose(2, 1, 0, 3)).reshape(nsl, 128, NCH * ncols)


def prep_shared(inp, layer_ids):
    sh = {}
    gl = []
    for i in range(DEPTH):
        gl += [inp[f"l{i}_attn_norm"], inp[f"l{i}_mlp_norm"]]
    gl.append(inp["final_norm"])
    g = np.stack([np.asarray(x, np.float32) for x in gl])
    sh["gains"] = np.ascontiguousarray(g.reshape(9, NCH, 128).transpose(2, 0, 1)).reshape(128, 9 * NCH)
    sh["ident"] = np.eye(128, dtype=np.float32)
    kk, qq = np.arange(128)[:, None], np.arange(256)[None, :]
    sh["band"] = np.where((qq >= kk) & (qq <= kk + 128), 0.0, -30000.0).astype(np.float32)
    sh["ropeA"], sh["ropeC"] = _rope_tables()
    qn, kn = np.asarray(inp["l2_q_norm"], np.float32), np.asarray(inp["l2_k_norm"], np.float32)
    sh["cgain"] = np.ascontiguousarray(np.broadcast_to(np.concatenate([qn, qn, qn, qn, kn, kn])[None, :], (128, 384)))
    sh["biasB"] = _bias_tables(np.asarray(inp["l1_rpb"], np.float32))
    for li in layer_ids:
        kind = li % 3
        sh[f"w{li}_in"] = _win_layout(np.asarray(inp[f"l{li}_w_in"], np.float32), kind)
        sh[f"w{li}_out"] = np.ascontiguousarray(np.asarray(inp[f"l{li}_w_out"], np.float32)).reshape(8, 128, D)
        wu = np.asarray(inp[f"l{li}_w_up"], np.float32).reshape(NCH, 128, 8, 512)
        sh[f"w{li}_up"] = np.ascontiguousarray(wu.transpose(2, 1, 0, 3)).reshape(8, 128, NCH * 512)
        wdn = np.asarray(inp[f"l{li}_w_down"], np.float32).reshape(8, 4, 128, D)
        sh[f"w{li}_dn"] = np.ascontiguousarray(wdn.transpose(0, 2, 1, 3)).reshape(8, 128, 4 * D)
    return sh


_CACHE = {}


def run_layers(inp, x, layer_ids, do_final, n_cores, dbg=None):
    key = (tuple(layer_ids), do_final, str(dbg))
    if key not in _CACHE:
        _CACHE[key] = build_program(list(layer_ids), do_final, dbg)[0]
    nc = _CACHE[key]
    sh = prep_shared(inp, layer_ids)
    in_maps = []
    for b in range(n_cores):
        m = dict(sh)
        m["xT"] = np.ascontiguousarray(np.asarray(x[b], np.float32).T)
        in_maps.append(m)
    res = run_bass_kernel_spmd(nc, in_maps, core_ids=list(range(n_cores)))
    return np.stack([np.ascontiguousarray(r["outT"].T) for r in res.results])


def kernel(**inputs):
    x = np.asarray(inputs["x"], np.float32)
    out = run_layers(inputs, x, [0, 1, 2, 3], True, 8)
    return out.astype(np.float32)
```
